# Optimizing a Trainium2 kernel written in Bass

```python
import jax, jax.numpy as jnp
from jax import lax
import numpy as np

D_MODEL = 1024
BATCH = 8
SEQ = 8192
DEPTH = 4

N_META = 16
MLSTM_HEADS = 4
MLSTM_DQK = 128
MLSTM_DV = 256
MLSTM_CHUNK = 64
QK_CONV_WIDTH = 4
GATE_SOFTCAP = 15.0
SB_HEADS = 4
SB_DH = 128
SB_BLOCK = 128
PAD_FRONT = SB_BLOCK - N_META
CONV_WIDTH = 31
FFN_HIDDEN = -(-8 * D_MODEL // (3 * 256)) * 256
MQK = MLSTM_HEADS * MLSTM_DQK
MV = MLSTM_HEADS * MLSTM_DV
SBW = SB_HEADS * SB_DH
IN_SIZES = (2 * MQK, MV, MV, 2 * MLSTM_HEADS, SBW, SBW, SBW)
IN_WIDTH = sum(IN_SIZES)
MIX_WIDTH = MV + SBW
N_EVEN = (DEPTH + 1) // 2
N_ODD = DEPTH // 2
NEG = -1e30
EPS = 1e-6

kernel_name = 'hybrid_mlstm_stickbreaking_conformer'


def rms_norm(x, g):
    xf = x.astype(jnp.float32)
    y = xf * lax.rsqrt(jnp.mean(xf * xf, axis=-1, keepdims=True) + EPS)
    return (y * g.astype(jnp.float32)).astype(x.dtype)


def layer_norm(x, g, b):
    xf = x.astype(jnp.float32)
    mu = jnp.mean(xf, axis=-1, keepdims=True)
    var = jnp.mean(jnp.square(xf - mu), axis=-1, keepdims=True)
    y = (xf - mu) * lax.rsqrt(var + EPS)
    return (y * g.astype(jnp.float32) + b.astype(jnp.float32)).astype(x.dtype)


def causal_depthwise_conv(x, w, b):
    k = w.shape[0]
    y = lax.conv_general_dilated(x, w[:, None, :].astype(x.dtype), (1,), [(k - 1, 0)],
                                 dimension_numbers=('NWC', 'WIO', 'NWC'),
                                 feature_group_count=x.shape[-1])
    return y + b.astype(x.dtype)


def to_chunks(a):
    b, h, t = a.shape[:3]
    a = a.reshape(b, h, t // MLSTM_CHUNK, MLSTM_CHUNK, *a.shape[3:])
    return jnp.moveaxis(a, 2, 0)


def mlstm_chunkwise(q, k, v, log_i, log_f):
    b, h, t, dk = q.shape
    dv = v.shape[-1]
    tril = jnp.tril(jnp.ones((MLSTM_CHUNK, MLSTM_CHUNK), dtype=bool))

    def step(carry, xs):
        c_st, n_st, m_st = carry
        qc, kc, vc, li, lf = xs
        bcum = jnp.cumsum(lf, axis=-1)
        g = bcum[..., -1]
        dmat = bcum[..., :, None] - bcum[..., None, :] + li[..., None, :]
        dmat = jnp.where(tril, dmat, NEG)
        inter = bcum + m_st[..., None]
        m_t = jnp.maximum(inter, jnp.max(dmat, axis=-1))
        w_intra = jnp.exp(dmat - m_t[..., None])
        w_inter = jnp.exp(inter - m_t)
        s = jnp.einsum('bhtd,bhsd->bhts', qc, kc) * w_intra
        num = jnp.einsum('bhts,bhsv->bhtv', s, vc) + w_inter[..., None] * jnp.einsum('bhtd,bhdv->bhtv', qc, c_st)
        den = jnp.sum(s, axis=-1) + w_inter * jnp.einsum('bhtd,bhd->bht', qc, n_st)
        h_out = num / jnp.maximum(jnp.abs(den), jnp.exp(-m_t))[..., None]
        a = g[..., None] - bcum + li
        m_new = jnp.maximum(g + m_st, jnp.max(a, axis=-1))
        wa = jnp.exp(a - m_new[..., None])
        wc = jnp.exp(g + m_st - m_new)
        c_new = wc[..., None, None] * c_st + jnp.einsum('bhs,bhsd,bhsv->bhdv', wa, kc, vc)
        n_new = wc[..., None] * n_st + jnp.einsum('bhs,bhsd->bhd', wa, kc)
        return (c_new, n_new, m_new), h_out

    init = (jnp.zeros((b, h, dk, dv), jnp.float32), jnp.zeros((b, h, dk), jnp.float32),
            jnp.zeros((b, h), jnp.float32))
    xs = (to_chunks(q), to_chunks(k), to_chunks(v), to_chunks(log_i), to_chunks(log_f))
    _, hs = lax.scan(step, init, xs)
    return jnp.moveaxis(hs, 0, 2).reshape(b, h, t, dv)


def stick_breaking(q, k, v, valid):
    b, h, t, d = q.shape
    nb = t // SB_BLOCK
    scale = d ** -0.5
    r = jnp.arange(SB_BLOCK)
    rev_in = (r[:, None] >= r[None, :]).astype(jnp.float32)
    outs = []
    for i in range(nb):
        nk = i + 1
        end = nk * SB_BLOCK
        qi = q[:, :, i * SB_BLOCK:end]
        kb = k[:, :, :end].reshape(b, h, nk, SB_BLOCK, d)
        vb = v[:, :, :end].reshape(b, h, nk, SB_BLOCK, d)
        z = jnp.einsum('bhtd,bhnsd->bhtns', qi, kb).astype(jnp.float32) * scale
        t_idx = i * SB_BLOCK + r
        s_idx = jnp.arange(end).reshape(nk, SB_BLOCK)
        mask = (s_idx[None] < t_idx[:, None, None]) & valid[:end].reshape(nk, SB_BLOCK)[None]
        log1m = jnp.where(mask, -jax.nn.softplus(z), 0.0)
        within = jnp.einsum('bhtns,su->bhtnu', log1m, rev_in, precision=lax.Precision.HIGHEST)
        n_r = jnp.arange(nk)
        rev_blk = (n_r[:, None] > n_r[None, :]).astype(jnp.float32)
        across = jnp.einsum('bhtn,nm->bhtm', within[..., 0], rev_blk, precision=lax.Precision.HIGHEST)
        log_w = jnp.where(mask, z + within + across[..., None], NEG)
        w = jnp.exp(log_w)
        outs.append(jnp.einsum('bhtns,bhnsd->bhtd', w, vb.astype(jnp.float32)))
    return jnp.concatenate(outs, axis=2)


def mlstm_sb_mixer(u, w_in, qk_conv_w, qk_conv_b, gate_b, hnorm_g, w_out):
    b, t, _ = u.shape
    tp = t + PAD_FRONT
    up = jnp.pad(u, ((0, 0), (PAD_FRONT, 0), (0, 0)))
    valid = jnp.arange(tp) >= PAD_FRONT
    proj = up @ w_in
    qk_m, v_m, o_m, gates, q_s, k_s, v_s = jnp.split(proj, np.cumsum(IN_SIZES)[:-1].tolist(), axis=-1)
    qk_m = jax.nn.silu(causal_depthwise_conv(qk_m, qk_conv_w, qk_conv_b))
    q_m, k_m = jnp.split(qk_m, 2, axis=-1)
    gates = gates.astype(jnp.float32) + gate_b.astype(jnp.float32)
    gates = GATE_SOFTCAP * jnp.tanh(gates / GATE_SOFTCAP)
    log_i = jnp.where(valid[:, None], gates[..., :MLSTM_HEADS], NEG)
    log_f = jnp.where(valid[:, None], jax.nn.log_sigmoid(gates[..., MLSTM_HEADS:]), 0.0)

    def heads(a, nh):
        return a.reshape(b, tp, nh, -1).transpose(0, 2, 1, 3)

    h_m = mlstm_chunkwise(heads(q_m, MLSTM_HEADS) * MLSTM_DQK ** -0.5, heads(k_m, MLSTM_HEADS),
                          heads(v_m, MLSTM_HEADS), log_i.transpose(0, 2, 1), log_f.transpose(0, 2, 1))
    h_m = rms_norm(h_m.transpose(0, 2, 1, 3), hnorm_g.reshape(MLSTM_HEADS, MLSTM_DV)).reshape(b, tp, MV)
    h_m = h_m * jax.nn.sigmoid(o_m.astype(jnp.float32))
    h_s = stick_breaking(heads(q_s, SB_HEADS), heads(k_s, SB_HEADS), heads(v_s, SB_HEADS), valid)
    h_s = h_s.transpose(0, 2, 1, 3).reshape(b, tp, SBW)
    mixed = jnp.concatenate([h_m, h_s], axis=-1)[:, PAD_FRONT:].astype(u.dtype)
    return mixed @ w_out


def conformer_conv(u, w_pw1, b_pw1, w_dw, b_dw, ln_g, ln_b, w_pw2, b_pw2):
    a, gate = jnp.split(u @ w_pw1 + b_pw1, 2, axis=-1)
    y = a * jax.nn.sigmoid(gate)
    y = causal_depthwise_conv(y, w_dw, b_dw)
    y = jax.nn.silu(layer_norm(y, ln_g, ln_b))
    return y @ w_pw2 + b_pw2


def swiglu(u, w_gate, w_up, w_down):
    return (jax.nn.silu(u @ w_gate) * (u @ w_up)) @ w_down


def setup_inputs(seed: int = 0) -> dict:
    key = jax.random.key(seed)
    ks = jax.random.split(key, 24)
    f32 = jnp.float32
    nrm = lambda k, shape, s: jax.random.normal(k, shape, f32) * s
    gate_noise = nrm(ks[5], (N_EVEN, 2 * MLSTM_HEADS), 0.1)
    gate_center = jnp.concatenate([jnp.full((MLSTM_HEADS,), -2.0, f32), jnp.full((MLSTM_HEADS,), 3.0, f32)])
    return {
        'x': nrm(ks[0], (BATCH, SEQ, D_MODEL), 1.0),
        'meta': nrm(ks[1], (N_META, D_MODEL), 1.0),
        'norm_g': 1.0 + nrm(ks[2], (DEPTH, 4, D_MODEL), 0.02),
        'mix_w_in': nrm(ks[3], (N_EVEN, D_MODEL, IN_WIDTH), D_MODEL ** -0.5),
        'mix_qk_conv_w': nrm(ks[4], (N_EVEN, QK_CONV_WIDTH, 2 * MQK), QK_CONV_WIDTH ** -0.5),
        'mix_qk_conv_b': nrm(ks[6], (N_EVEN, 2 * MQK), 0.02),
        'mix_gate_b': gate_center + gate_noise,
        'mix_hnorm_g': 1.0 + nrm(ks[7], (N_EVEN, MV), 0.02),
        'mix_w_out': nrm(ks[8], (N_EVEN, MIX_WIDTH, D_MODEL), MIX_WIDTH ** -0.5),
        'conv_w_pw1': nrm(ks[9], (N_ODD, D_MODEL, 2 * D_MODEL), D_MODEL ** -0.5),
        'conv_b_pw1': nrm(ks[10], (N_ODD, 2 * D_MODEL), 0.02),
        'conv_w_dw': nrm(ks[11], (N_ODD, CONV_WIDTH, D_MODEL), CONV_WIDTH ** -0.5),
        'conv_b_dw': nrm(ks[12], (N_ODD, D_MODEL), 0.02),
        'conv_ln_g': 1.0 + nrm(ks[13], (N_ODD, D_MODEL), 0.02),
        'conv_ln_b': nrm(ks[14], (N_ODD, D_MODEL), 0.02),
        'conv_w_pw2': nrm(ks[15], (N_ODD, D_MODEL, D_MODEL), D_MODEL ** -0.5),
        'conv_b_pw2': nrm(ks[16], (N_ODD, D_MODEL), 0.02),
        'ffn_w_gate': nrm(ks[17], (DEPTH, D_MODEL, FFN_HIDDEN), D_MODEL ** -0.5),
        'ffn_w_up': nrm(ks[18], (DEPTH, D_MODEL, FFN_HIDDEN), D_MODEL ** -0.5),
        'ffn_w_down': nrm(ks[19], (DEPTH, FFN_HIDDEN, D_MODEL), FFN_HIDDEN ** -0.5),
    }


def reference(x, meta, norm_g, mix_w_in, mix_qk_conv_w, mix_qk_conv_b, mix_gate_b, mix_hnorm_g,
              mix_w_out, conv_w_pw1, conv_b_pw1, conv_w_dw, conv_b_dw, conv_ln_g, conv_ln_b,
              conv_w_pw2, conv_b_pw2, ffn_w_gate, ffn_w_up, ffn_w_down):
    b = x.shape[0]
    h = jnp.concatenate([jnp.broadcast_to(meta[None].astype(x.dtype), (b, N_META, D_MODEL)), x], axis=1)
    for layer in range(DEPTH):
        g = norm_g[layer]
        u = rms_norm(h, g[0])
        i = layer // 2
        if layer % 2 == 0:
            y = mlstm_sb_mixer(u, mix_w_in[i], mix_qk_conv_w[i], mix_qk_conv_b[i], mix_gate_b[i],
                               mix_hnorm_g[i], mix_w_out[i])
        else:
            y = conformer_conv(u, conv_w_pw1[i], conv_b_pw1[i], conv_w_dw[i], conv_b_dw[i],
                               conv_ln_g[i], conv_ln_b[i], conv_w_pw2[i], conv_b_pw2[i])
        h = h + rms_norm(y, g[1])
        f = swiglu(rms_norm(h, g[2]), ffn_w_gate[layer], ffn_w_up[layer], ffn_w_down[layer])
        h = h + rms_norm(f, g[3])
    return h[:, N_META:]
```

```python
import numpy as np
import ml_dtypes
from contextlib import ExitStack
import concourse.bass as bass
import concourse.mybir as mybir
from concourse.bass_utils import run_bass_kernel_spmd

F32 = mybir.dt.float32
BF16 = mybir.dt.bfloat16
AF = mybir.ActivationFunctionType
ALU = mybir.AluOpType

D = 1024
FF = 2816
NFC = FF // 128
EPS = 1e-6
NDMA = 40
INW = 4616


class Buf:
    __slots__ = ("name", "excl", "last_w", "readers")

    def __init__(self, name, excl=False):
        self.name = name
        self.excl = excl
        self.last_w = None
        self.readers = {}


class V:
    __slots__ = ("buf", "ap")

    def __init__(self, buf, ap):
        self.buf = buf
        self.ap = ap

    def __getitem__(self, k):
        return V(self.buf, self.ap[k])

    def re(self, pat, **kw):
        return V(self.buf, self.ap.rearrange(pat, **kw))


class Op:
    __slots__ = ("eng", "fn", "deps", "dma", "slot", "ninc", "tok", "idx")

    def __init__(self, eng, fn, dma):
        self.eng = eng
        self.fn = fn
        self.dma = dma
        self.deps = ()
        self.slot = -1
        self.ninc = False
        self.tok = None


ENGS = ("pe", "act", "dve", "pool", "sp")


class Sched:
    def __init__(self, nc):
        self.nc = nc
        self.ops = {e: [] for e in ENGS}
        self.pending = {e: set() for e in ENGS}
        self.dma_last = [None] * NDMA
        self.rr = 0
        self.last = {e: None for e in ENGS}
        self.nall = 0

    def op(self, eng, fn, reads=(), writes=(), dma=False):
        o = Op(eng, fn, dma)
        o.idx = self.nall
        self.nall += 1
        deps = {}

        def add(d):
            if d is None or d is o:
                return
            if d.dma:
                deps[("d", d.idx)] = d
            else:
                if d.eng == "pe" and eng == "pe" and not dma:
                    return
                k = ("c", d.eng)
                if k not in deps or deps[k].idx < d.idx:
                    deps[k] = d

        rb, wb = [], []
        for v in reads:
            b = v.buf if isinstance(v, V) else v
            (wb if b.excl else rb).append(b)
        for v in writes:
            b = v.buf if isinstance(v, V) else v
            wb.append(b)
        for b in rb:
            add(b.last_w)
        for b in wb:
            add(b.last_w)
            for r in b.readers.values():
                add(r)
        for d in self.pending[eng]:
            add(d)
        self.pending[eng] = set()
        if dma:
            o.slot = self.rr
            self.rr = (self.rr + 1) % NDMA
            add(self.dma_last[o.slot])
            self.dma_last[o.slot] = o
        o.deps = tuple(deps.values())
        for b in rb:
            key = ("d", o.idx) if dma else eng
            b.readers[key] = o
        for b in wb:
            b.last_w = o
            b.readers = {}
        self.ops[eng].append(o)
        if not dma:
            self.last[eng] = o
        return o

    def barrier(self):
        deps_r = []
        o = Op("sp", lambda e: e.drain(), False)
        o.idx = self.nall
        self.nall += 1
        deps = {}
        for e in ENGS:
            if self.last[e] is not None:
                deps[("c", e)] = self.last[e]
        for d in self.dma_last:
            if d is not None:
                deps[("d", d.idx)] = d
        for d in self.pending["sp"]:
            deps[("x", d.idx)] = d
        o.deps = tuple(deps.values())
        self.ops["sp"].append(o)
        self.last["sp"] = o
        for e in ENGS:
            self.pending[e] = {o}
        return o

    def emit(self, stack):
        nc = self.nc
        for e in ENGS:
            for o in self.ops[e]:
                for d in o.deps:
                    d.ninc = True
        esem = {e: stack.enter_context(nc.semaphore("s_" + e)) for e in ENGS}
        dsem = [stack.enter_context(nc.semaphore("d_%d" % i)) for i in range(NDMA)]
        dcnt = [0] * NDMA
        allops = []
        for e in ENGS:
            allops.extend(self.ops[e])
        allops.sort(key=lambda o: o.idx)
        ecnt = {e: 0 for e in ENGS}
        for o in allops:
            if o.dma:
                dcnt[o.slot] += 16
                o.tok = (dsem[o.slot], dcnt[o.slot], ("d", o.slot))
            elif o.ninc:
                ecnt[o.eng] += 1
                o.tok = (esem[o.eng], ecnt[o.eng], ("e", o.eng))
        handles = {"pe": "tensor", "act": "scalar", "dve": "vector", "pool": "gpsimd", "sp": "sync"}
        block = stack.enter_context(nc.Block())

        def run(eng_name):
            def body(engine):
                seen = {}
                for o in self.ops[eng_name]:
                    waits = {}
                    for d in o.deps:
                        sem, val, key = d.tok
                        if key not in waits or waits[key][1] < val:
                            waits[key] = (sem, val)
                    for key, (sem, val) in waits.items():
                        if seen.get(key, 0) < val:
                            engine.wait_ge(sem, val)
                            seen[key] = val
                    ins = o.fn(engine)
                    if o.tok is not None:
                        ins.then_inc(o.tok[0], 16 if o.dma else 1)
            return body

        block.tensor(run("pe"))
        block.scalar(run("act"))
        block.vector(run("dve"))
        block.gpsimd(run("pool"))
        block.sync(run("sp"))


def _ap(x):
    return x.ap if isinstance(x, V) else x


class K:
    def __init__(self, nc, S):
        self.nc = nc
        self.S = S
        self.n = 0

    def name(self, p):
        self.n += 1
        return "%s_%d" % (p, self.n)

    def sb(self, stack, name, shape, dt):
        t = stack.enter_context(self.nc.sbuf_tensor(self.name(name), list(shape), dt))
        return V(Buf(name), t[:] if len(shape) == 2 else t[:])

    def ps(self, stack, name, shape, dt):
        t = stack.enter_context(self.nc.psum_tensor(self.name(name), list(shape), dt))
        return V(Buf(name, excl=True), t[:])

    def dma(self, out, in_, q="sp", **kw):
        o_, i_ = out.ap, in_.ap
        return self.S.op(q, lambda e: e.dma_start(out=o_, in_=i_, **kw), [in_], [out], dma=True)

    def mm(self, out, pairs, start=True, stop=True, extra_reads=()):
        o_ = out.ap
        pr = [(a.ap, b.ap) for a, b in pairs]
        n = len(pr)

        def fn(e):
            ins = None
            for i, (a, b) in enumerate(pr):
                ins = e.matmul(o_, lhsT=a, rhs=b, start=(start and i == 0), stop=(stop and i == n - 1))
            return ins

        reads = [x for p in pairs for x in p] + list(extra_reads)
        return self.S.op("pe", fn, reads, [out])

    def transposes(self, outs_ins, ident):
        pr = [(o.ap, i.ap) for o, i in outs_ins]
        id_ = ident.ap

        def fn(e):
            ins = None
            for o, i in pr:
                ins = e.transpose(o, i, id_)
            return ins

        return self.S.op("pe", fn, [i for _, i in outs_ins] + [ident], [o for o, _ in outs_ins])

    def act(self, out, in_, func, bias=None, scale=None, accum=None):
        kw = {}
        reads = [in_]
        writes = [out]
        if bias is not None:
            kw["bias"] = _ap(bias)
            if isinstance(bias, V):
                reads.append(bias)
        if scale is not None:
            kw["scale"] = _ap(scale)
            if isinstance(scale, V):
                reads.append(scale)
        if accum is not None:
            kw["accum_out"] = accum.ap
            writes.append(accum)
        o_, i_ = out.ap, in_.ap
        return self.S.op("act", lambda e: e.activation(out=o_, in_=i_, func=func, **kw), reads, writes)

    def tt(self, out, a, b, op, eng="dve"):
        o_, a_, b_ = out.ap, a.ap, b.ap
        return self.S.op(eng, lambda e: e.tensor_tensor(out=o_, in0=a_, in1=b_, op=op), [a, b], [out])

    def ts(self, out, a, s1, op0, s2=None, op1=None, eng="dve", accum=None):
        reads = [a]
        for s in (s1, s2):
            if isinstance(s, V):
                reads.append(s)
        o_, a_ = out.ap, a.ap
        s1_, s2_ = _ap(s1), _ap(s2)
        kw = {}
        writes = [out]
        if op1 is not None:
            kw["op1"] = op1
        if accum is not None:
            kw["accum_out"] = accum.ap
            writes.append(accum)
        return self.S.op(eng, lambda e: e.tensor_scalar(out=o_, in0=a_, scalar1=s1_, scalar2=s2_, op0=op0, **kw),
                         reads, writes)

    def stt(self, out, a, s, b, op0, op1):
        reads = [a, b]
        if isinstance(s, V):
            reads.append(s)
        o_, a_, b_, s_ = out.ap, a.ap, b.ap, _ap(s)
        return self.S.op("dve", lambda e: e.scalar_tensor_tensor(out=o_, in0=a_, scalar=s_, in1=b_, op0=op0, op1=op1),
                         reads, [out])

    def copy(self, out, in_, eng="dve"):
        o_, i_ = out.ap, in_.ap
        if eng == "act":
            return self.S.op("act", lambda e: e.activation(out=o_, in_=i_, func=AF.Copy), [in_], [out])
        return self.S.op(eng, lambda e: e.tensor_copy(out=o_, in_=i_), [in_], [out])

    def memset(self, out, val, eng="dve"):
        o_ = out.ap
        return self.S.op(eng, lambda e: e.memset(o_, val), [], [out])

    def recip(self, out, in_):
        o_, i_ = out.ap, in_.ap
        return self.S.op("dve", lambda e: e.reciprocal(out=o_, in_=i_), [in_], [out])


def _mk_cols():
    off = {}
    n = 0

    def add(name, w):
        nonlocal n
        off[name] = (n, w)
        n += w

    add("ng", 16 * 8)
    add("valid", 1)
    for i in range(2):
        add("b1_%d" % i, 16)
        add("wdw_%d" % i, 8 * 31)
        add("bdw_%d" % i, 8)
        add("lng_%d" % i, 8)
        add("lnb_%d" % i, 8)
        add("cw_%d" % i, 32)
        add("cb_%d" % i, 8)
        add("hng_%d" % i, 8)
    return off, n


COLS, NCOLS = _mk_cols()
NROWS = 20
SB_SCALE = 128 ** -0.5
LN_QSCALE = float(np.log(128 ** -0.5))


class Prog:
    def __init__(self, NB, layers, debug=False):
        self.NB = NB
        self.NP = NB * 128
        self.layers = layers
        self.debug = debug
        nc = bass.Bass("TRN2", target_bir_lowering=False)
        self.nc = nc
        self.S = Sched(nc)
        self.k = K(nc, self.S)
        self.cast_rr = 0

    def din(self, name, shape, dt=F32):
        return self.nc.dram_tensor(name, list(shape), dt, kind="ExternalInput").ap()

    def dscr(self, name, shape, dt, out=False):
        kind = "ExternalOutput" if (out or self.debug) else "Internal"
        return self.nc.dram_tensor(name, list(shape), dt, kind=kind).ap()

    def col(self, name, a=0, b=None):
        o, w = COLS[name]
        if b is None:
            b = w
        return self.colsT[:, o + a:o + b]

    def build(self):
        nc, k, S = self.nc, self.k, self.S
        NB, NP = self.NB, self.NP
        I = {}
        I["hp"] = self.din("hp", [NP, D])
        I["colsT"] = self.din("colsT", [128, NCOLS])
        I["rows"] = self.din("rows", [NROWS, D])
        I["ffn_w_gate"] = self.din("ffn_w_gate", [4, D, FF])
        I["ffn_w_up"] = self.din("ffn_w_up", [4, D, FF])
        I["ffn_w_down"] = self.din("ffn_w_down", [4, FF, D])
        I["conv_w_pw1"] = self.din("conv_w_pw1", [2, D, 2 * D])
        I["conv_w_pw2"] = self.din("conv_w_pw2", [2, D, D])
        I["mix_w_in"] = self.din("mix_w_in", [2, D, INW])
        I["mix_w_out"] = self.din("mix_w_out", [2, 1536, D])
        I["c_identb"] = self.din("c_identb", [128, 128], BF16)
        I["c_onesb"] = self.din("c_onesb", [128, 128], BF16)
        I["c_trib"] = self.din("c_trib", [128, 128], BF16)
        I["c_maskT"] = self.din("c_maskT", [128, 128], BF16)
        I["c_maskJ"] = self.din("c_maskJ", [128, 4 * 512], BF16)
        I["c_trif"] = self.din("c_trif", [128, 128], F32)
        I["c_onesf"] = self.din("c_onesf", [128, 128], F32)
        self.I = I
        out = self.dscr("out", [NP - 128, D], F32, out=True)
        hs = self.dscr("hs", [NP, D], F32)
        self.d_mm = self.dscr("s_mm", [NP, D], BF16)
        self.d_qs = self.dscr("s_qs", [4, 128, NP], BF16)
        self.d_ks = self.dscr("s_ks", [4, 128, NP], BF16)
        self.d_vs = self.dscr("s_vs", [NP, 512], BF16)
        self.d_hs = self.dscr("s_hs", [4, 128, NP], BF16)
        self.hp_blk = [V(Buf("hp%d" % b), I["hp"][b * 128:(b + 1) * 128, :]) for b in range(NB)]
        self.hs_blk = [V(Buf("hs%d" % b), hs[b * 128:(b + 1) * 128, :]) for b in range(NB)]
        self.out_blk = [None] + [V(Buf("out%d" % b), out[(b - 1) * 128:b * 128, :]) for b in range(1, NB)]
        self.b_mm = [Buf("mm%d" % b) for b in range(NB)]
        self.b_q = [Buf("q%d" % b) for b in range(NB)]
        self.b_hs = [Buf("hsT%d" % b) for b in range(NB)]
        self.hs_first = [0] + [((b - 1) // 4) * 4 + 1 for b in range(1, NB)]
        self.csrc = Buf("csrc")

        with ExitStack() as gs:
            self.colsT = k.sb(gs, "colsT", [128, NCOLS], F32)
            self.identb = k.sb(gs, "identb", [128, 128], BF16)
            self.epsc = k.sb(gs, "epsc", [128, 1], F32)
            self.junk = k.sb(gs, "junk", [128, D], BF16)
            k.dma(self.colsT, V(self.csrc, I["colsT"]))
            k.dma(self.identb, V(self.csrc, I["c_identb"]))
            k.memset(self.epsc, EPS)
            self.onec = k.sb(gs, "onec", [128, 1], F32)
            k.memset(self.onec, 1.0)
            self.lnq = k.sb(gs, "lnq", [128, 1], F32)
            k.memset(self.lnq, LN_QSCALE)
            self.valid = self.col("valid")
            cur_in = self.hp_blk
            nl = len(self.layers)
            for li, (kind, L) in enumerate(self.layers):
                last = li == nl - 1
                dst = self.out_blk if last else self.hs_blk
                if kind == "F":
                    self.phase_ffn(L, cur_in, dst)
                elif kind == "V":
                    self.phase_conf(L, cur_in, dst)
                elif kind == "M":
                    self.phase_mixA(L, cur_in)
                    S.barrier()
                    self.phase_mixB(L)
                    S.barrier()
                    self.phase_mixC(L, cur_in, dst)
                elif kind == "MA":
                    self.phase_mixA(L, cur_in)
                elif kind == "MB":
                    self.phase_mixB(L)
                elif kind == "MC":
                    self.phase_mixC(L, cur_in, dst)
                if kind != "MA" and kind != "MB":
                    cur_in = self.hs_blk
                S.barrier()
            S.barrier()
            S.emit(gs)
        return nc

    def wstage(self, st, piece=2048, n=6):
        self.stg = [self.k.sb(st, "stg%d" % i, [128, piece], F32) for i in range(n)]
        self.stg_i = 0
        self.piece = piece

    def load_w(self, dst, src, nk, ncols, gcol=None, c_lo=0):
        k = self.k
        piece = self.piece
        for kc in range(nk):
            for c0 in range(0, ncols, piece):
                c1 = min(ncols, c0 + piece)
                stg = self.stg[self.stg_i % len(self.stg)]
                self.stg_i += 1
                k.dma(stg[:, 0:c1 - c0], V(self.csrc, src[kc * 128:(kc + 1) * 128, c0:c1]))
                eng = ("dve", "act")[self.cast_rr % 2]
                self.cast_rr += 1
                o = dst[:, kc, c_lo + c0:c_lo + c1]
                i = stg[:, 0:c1 - c0]
                g = None if (gcol is None or gcol[kc] is None) else gcol[kc]
                if g is None:
                    k.copy(o, i, eng=eng)
                elif eng == "act":
                    k.act(o, i, AF.Copy, scale=g)
                else:
                    k.ts(o, i, g, ALU.mult, eng=eng)

    def gcols(self, L, j):
        o = (L * 4 + j) * 8
        return [self.col("ng", o + c, o + c + 1) for c in range(8)]

    def bcast_row(self, dst, r, n=D):
        self.k.dma(dst, V(self.csrc, self.I["rows"][r:r + 1, 0:n].partition_broadcast(128)))

    def rstd(self, out, ss, scale, tmp):
        k = self.k
        k.act(tmp, ss, AF.Ln, bias=self.epsc, scale=scale)
        k.act(out, tmp, AF.Exp, scale=-0.5)

    def prenorm_a(self, hblk, ub, sm):
        k = self.k
        ss, t1, rs = sm[:, 0:1], sm[:, 1:2], sm[:, 2:3]
        k.act(self.junk, hblk, AF.Square, accum=ss)
        self.rstd(rs, ss, 1.0 / D, t1)
        k.ts(ub, hblk, rs, ALU.mult)

    def prenorm_b(self, ub, pT, uT_dst, eng="act"):
        k = self.k
        k.transposes([(pT[:, c, :], ub[:, c * 128:(c + 1) * 128]) for c in range(8)], self.identb)
        k.copy(uT_dst, pT, eng=eng)

    def make_pre(self, tl, src, hA, ub, smA, uT, pT, eng="act"):
        k = self.k

        def pre_a(ti):
            for bi, b in enumerate(tl[ti]):
                k.dma(hA[bi % 2], src[b])
                self.prenorm_a(hA[bi % 2], ub[bi % 2], smA[bi % 2])

        def pre_b(ti):
            u = uT[ti % 2]
            for bi, b in enumerate(tl[ti]):
                self.prenorm_b(ub[bi % 2], pT, u[:, :, bi * 128:(bi + 1) * 128], eng=eng)

        return pre_a, pre_b

    def postnorm_res(self, ys, hblk, gB, sm, tmp, b):
        k = self.k
        ss0, ss1, ss, t1, rs = (sm[:, i:i + 1] for i in range(5))
        k.act(self.junk[:, 0:512], ys[0], AF.Square, accum=ss0)
        k.act(self.junk[:, 512:1024], ys[1], AF.Square, accum=ss1)
        k.tt(ss, ss0, ss1, ALU.add)
        self.rstd(rs, ss, 1.0 / D, t1)
        if b == 0:
            k.tt(rs, rs, self.valid, ALU.mult)
        for hf in range(2):
            k.stt(tmp[:, hf * 512:(hf + 1) * 512], ys[hf], rs, gB[:, hf * 512:(hf + 1) * 512], ALU.mult, ALU.mult)
        k.tt(hblk, hblk, tmp, ALU.add)

    def tiles(self, nbt):
        NB = self.NB
        return [list(range(t * nbt, min(NB, (t + 1) * nbt))) for t in range((NB + nbt - 1) // nbt)]

    def phase_ffn(self, L, src, dst, TT=256):
        k, I = self.k, self.I
        NSPLIT = 18
        with ExitStack() as st:
            Wg = k.sb(st, "Wg", [128, 8, FF], BF16)
            Wu = k.sb(st, "Wu", [128, 8, FF], BF16)
            Wd = k.sb(st, "Wd", [128, NFC, D], BF16)
            gB = k.sb(st, "gB", [128, D], F32)
            self.bcast_row(gB, L * 4 + 3)
            g2 = self.gcols(L, 2)
            with ExitStack() as ws:
                self.wstage(ws)
                self.load_w(Wg, I["ffn_w_gate"][L], 8, FF, g2)
                self.load_w(Wu, I["ffn_w_up"][L], 8, FF, g2)
                self.load_w(Wd, I["ffn_w_down"][L], NFC, D, None)
            self.S.barrier()
            uT = [k.sb(st, "uT%d" % i, [128, 8, TT], BF16) for i in range(2)]
            aT1 = k.sb(st, "aT1", [128, NSPLIT, TT], BF16)
            aT2 = k.sb(st, "aT2", [128, NFC - NSPLIT, TT], BF16)
            hA = [k.sb(st, "hA%d" % i, [128, D], F32) for i in range(2)]
            hB = [k.sb(st, "hB%d" % i, [128, D], F32) for i in range(2)]
            ub = [k.sb(st, "ub%d" % i, [128, D], BF16) for i in range(2)]
            sg = [k.sb(st, "sg%d" % i, [128, TT], F32) for i in range(2)]
            tmp = k.sb(st, "tmp", [128, D], F32)
            smA = [k.sb(st, "smA%d" % i, [128, 8], F32) for i in range(2)]
            smB = [k.sb(st, "smB%d" % i, [128, 8], F32) for i in range(2)]
            pT = k.ps(st, "pT", [128, 8, 128], BF16)
            pgu = [k.ps(st, "pgu%d" % i, [128, 512], F32) for i in range(3)]
            py = [k.ps(st, "py%d" % i, [128, 512], F32) for i in range(4)]

            def aT(f):
                return aT1[:, f] if f < NSPLIT else aT2[:, f - NSPLIT]

            tl = self.tiles(TT // 128)
            pre_a, pre_b = self.make_pre(tl, src, hA, ub, smA, uT, pT)
            pre_a(0)
            pre_b(0)
            for ti, blocks in enumerate(tl):
                tt = len(blocks) * 128
                u = uT[ti % 2]
                if ti + 1 < len(tl):
                    pre_a(ti + 1)
                for f in range(NFC):
                    g_ = pgu[f % 3][:, 0:tt]
                    u_ = pgu[f % 3][:, 256:256 + tt]
                    k.mm(g_, [(Wg[:, kc, f * 128:(f + 1) * 128], u[:, kc, 0:tt]) for kc in range(8)])
                    k.mm(u_, [(Wu[:, kc, f * 128:(f + 1) * 128], u[:, kc, 0:tt]) for kc in range(8)])
                    s_ = sg[f % 2][:, 0:tt]
                    k.act(s_, g_, AF.Silu)
                    k.tt(aT(f)[:, 0:tt], s_, u_, ALU.mult)
                if ti + 1 < len(tl):
                    pre_b(ti + 1)
                for bi, b in enumerate(blocks):
                    h = hB[b % 2]
                    k.dma(h, src[b])
                    yy = [py[2 * (b % 2)], py[2 * (b % 2) + 1]]
                    bs = slice(bi * 128, (bi + 1) * 128)
                    for hf in range(2):
                        cs = slice(hf * 512, (hf + 1) * 512)
                        k.mm(yy[hf], [(aT1[:, f, bs], Wd[:, f, cs]) for f in range(NSPLIT)], start=True, stop=False)
                    for hf in range(2):
                        cs = slice(hf * 512, (hf + 1) * 512)
                        k.mm(yy[hf], [(aT2[:, f - NSPLIT, bs], Wd[:, f, cs]) for f in range(NSPLIT, NFC)],
                             start=False, stop=True)
                    self.postnorm_res(yy, h, gB, smB[b % 2], tmp, b)
                    if dst[b] is not None:
                        k.dma(dst[b], h, q="pool")

    def phase_conf(self, L, src, dst, TT=256):
        k, I = self.k, self.I
        i = L // 2
        with ExitStack() as st:
            W1 = k.sb(st, "W1", [128, 8, 2 * D], BF16)
            W2 = k.sb(st, "W2", [128, 8, D], BF16)
            Dg = k.sb(st, "Dg", [128, 8 * 31, 128], BF16)
            gB = k.sb(st, "gB", [128, D], F32)
            b2B = k.sb(st, "b2B", [128, D], F32)
            onesb = k.sb(st, "onesb", [128, 128], BF16)
            with ExitStack() as ws:
                self.wstage(ws)
                self.load_w(W1, I["conv_w_pw1"][i], 8, 2 * D, self.gcols(L, 0))
                self.load_w(W2, I["conv_w_pw2"][i], 8, D, None)
            self.S.barrier()
            uT = [k.sb(st, "uT%d" % x, [128, 8, TT], BF16) for x in range(2)]
            Y = [k.sb(st, "Y%d" % j, [128, 30 + TT], BF16) for j in range(8)]
            Z = k.sb(st, "Z", [128, 8, TT], F32)
            zb = k.sb(st, "zb", [128, 8, TT], BF16)
            zq = k.sb(st, "zq", [128, 8, TT], BF16)
            Nn = k.sb(st, "Nn", [128, 8, TT], BF16)
            m_ = k.sb(st, "m_", [128, TT], F32)
            msq = k.sb(st, "msq", [128, TT], F32)
            var = k.sb(st, "var", [128, TT], F32)
            rs_ = k.sb(st, "rs_", [128, TT], F32)
            tz = [k.sb(st, "tz%d" % x, [128, TT], F32) for x in range(2)]
            sg = [k.sb(st, "sg%d" % x, [128, TT], F32) for x in range(2)]
            hA = [k.sb(st, "hA%d" % x, [128, D], F32) for x in range(2)]
            hB = [k.sb(st, "hB%d" % x, [128, D], F32) for x in range(2)]
            ub = [k.sb(st, "ub%d" % x, [128, D], BF16) for x in range(2)]
            yb = [k.sb(st, "yb%d" % x, [128, D], F32) for x in range(2)]
            tmp = [k.sb(st, "tmp%d" % x, [128, D], F32) for x in range(2)]
            smA = [k.sb(st, "smA%d" % x, [128, 8], F32) for x in range(2)]
            smB = [k.sb(st, "smB%d" % x, [128, 8], F32) for x in range(2)]
            pT = k.ps(st, "pT", [128, 8, 128], BF16)
            pag = [k.ps(st, "pag%d" % x, [128, 512], F32) for x in range(2)]
            pcv = [k.ps(st, "pcv%d" % x, [128, 512], F32) for x in range(2)]
            pss = k.ps(st, "pss", [128, 512], F32)
            py = [k.ps(st, "py%d" % x, [128, 512], F32) for x in range(2)]

            self.bcast_row(gB, L * 4 + 1)
            self.bcast_row(b2B, 16 + i)
            b1h = k.sb(st, "b1h", [128, 8], F32)
            k.ts(b1h, self.col("b1_%d" % i, 8, 16), 0.5, ALU.mult)
            k.dma(onesb, V(self.csrc, I["c_onesb"]))
            for j in range(8):
                k.memset(Y[j][:, 0:30], 0.0, eng="pool")
                for t in range(31):
                    k.ts(Dg[:, j * 31 + t, :], self.identb, self.col("wdw_%d" % i, j * 31 + t, j * 31 + t + 1),
                         ALU.mult)

            tl = self.tiles(TT // 128)
            pre_a, pre_b = self.make_pre(tl, src, hA, ub, smA, uT, pT)

            def agA(ti, j):
                tt = len(tl[ti]) * 128
                u = uT[ti % 2]
                a_ = pag[j % 2][:, 0:tt]
                g_ = pag[j % 2][:, 256:256 + tt]
                k.mm(a_, [(W1[:, kc, j * 128:(j + 1) * 128], u[:, kc, 0:tt]) for kc in range(8)])
                k.mm(g_, [(W1[:, kc, D + j * 128:D + (j + 1) * 128], u[:, kc, 0:tt]) for kc in range(8)])
                k.act(sg[j % 2][:, 0:tt], g_, AF.Tanh, bias=b1h[:, j:j + 1], scale=0.5)

            def agB(ti, j):
                tt = len(tl[ti]) * 128
                a_ = pag[j % 2][:, 0:tt]
                s_ = sg[j % 2][:, 0:tt]
                k.ts(s_, s_, 0.5, ALU.mult, 0.5, ALU.add)
                k.stt(Y[j][:, 30:30 + tt], a_, self.col("b1_%d" % i, j, j + 1), s_, ALU.add, ALU.mult)
                if ti == 0:
                    k.memset(Y[j][:, 30:30 + 112], 0.0, eng="dve")

            def cv(ti, j):
                tt = len(tl[ti]) * 128
                c_ = pcv[j % 2][:, 0:tt]
                k.mm(c_, [(Dg[:, j * 31 + t, :], Y[j][:, t:t + tt]) for t in range(31)])
                bd = self.col("bdw_%d" % i, j, j + 1)
                k.act(Z[:, j, 0:tt], c_, AF.Identity, bias=bd)
                k.act(zb[:, j, 0:tt], c_, AF.Identity, bias=bd)
                k.act(zq[:, j, 0:tt], c_, AF.Square, bias=bd)
                k.copy(Y[j][:, 0:30], Y[j][:, tt:tt + 30], eng="dve")

            def prea_steps(ti):
                S_ = []
                for bi, b in enumerate(tl[ti]):
                    h, u_, sm = hA[bi % 2], ub[bi % 2], smA[bi % 2]
                    ss, t1, rs = sm[:, 0:1], sm[:, 1:2], sm[:, 2:3]

                    def p1(h=h, b=b, ss=ss):
                        k.dma(h, src[b])
                        k.act(self.junk, h, AF.Square, accum=ss)
                    S_.append(p1)
                    S_.append(lambda ss=ss, t1=t1, rs=rs: self.rstd(rs, ss, 1.0 / D, t1))
                    S_.append(lambda h=h, u_=u_, rs=rs: k.ts(u_, h, rs, ALU.mult))
                return S_

            def post_steps(ti):
                S_ = []
                for bi, b in enumerate(tl[ti]):
                    h, y_, tm, sm = hB[b % 2], yb[b % 2], tmp[b % 2], smB[b % 2]
                    ss0, ss1, ss, t1, rs = (sm[:, x:x + 1] for x in range(5))

                    def q1(y_=y_, ss0=ss0, ss1=ss1):
                        k.act(self.junk[:, 0:512], y_[:, 0:512], AF.Square, accum=ss0)
                        k.act(self.junk[:, 512:1024], y_[:, 512:1024], AF.Square, accum=ss1)

                    def q2(b=b, ss0=ss0, ss1=ss1, ss=ss, t1=t1, rs=rs):
                        k.tt(ss, ss0, ss1, ALU.add)
                        self.rstd(rs, ss, 1.0 / D, t1)
                        if b == 0:
                            k.tt(rs, rs, self.valid, ALU.mult)

                    def q3(y_=y_, tm=tm, rs=rs):
                        for hf in range(2):
                            cs = slice(hf * 512, (hf + 1) * 512)
                            k.stt(tm[:, cs], y_[:, cs], rs, gB[:, cs], ALU.mult, ALU.mult)

                    def q4(b=b, h=h, tm=tm):
                        k.tt(h, h, tm, ALU.add)
                        if dst[b] is not None:
                            k.dma(dst[b], h, q="pool")
                    S_ += [q1, q2, q3, q4]
                return S_

            pre_a(0)
            pre_b(0)
            for j in range(8):
                agA(0, j)
                agB(0, j)
            pending = []
            for ti, blocks in enumerate(tl):
                tt = len(blocks) * 128
                nxt = ti + 1 < len(tl)
                steps = pending + (prea_steps(ti + 1) if nxt else [])
                ns = len(steps)
                si = 0
                for j in range(8):
                    cv(ti, j)
                    upto = (ns * (j + 1)) // 8
                    while si < upto:
                        steps[si]()
                        si += 1
                if nxt:
                    pre_b(ti + 1)
                s1 = pss[:, 0:tt]
                s2 = pss[:, 256:256 + tt]
                k.mm(s1, [(onesb, zb[:, j, 0:tt]) for j in range(8)])
                k.mm(s2, [(onesb, zq[:, j, 0:tt]) for j in range(8)])
                mt, qt, vt, rt = m_[:, 0:tt], msq[:, 0:tt], var[:, 0:tt], rs_[:, 0:tt]
                k.act(mt, s1, AF.Copy, scale=1.0 / D)
                k.tt(qt, mt, mt, ALU.mult)
                k.stt(vt, s2, 1.0 / D, qt, ALU.mult, ALU.subtract)
                k.ts(vt, vt, 0.0, ALU.max)
                self.rstd(rt, vt, 1.0, qt)
                if nxt:
                    agA(ti + 1, 0)
                for j in range(8):
                    if nxt and j + 1 < 8:
                        agA(ti + 1, j + 1)
                    t_ = tz[j % 2][:, 0:tt]
                    k.tt(t_, Z[:, j, 0:tt], mt, ALU.subtract)
                    k.tt(t_, t_, rt, ALU.mult)
                    k.act(Nn[:, j, 0:tt], t_, AF.Silu, bias=self.col("lnb_%d" % i, j, j + 1),
                          scale=self.col("lng_%d" % i, j, j + 1))
                    if nxt:
                        agB(ti + 1, j)
                for bi, b in enumerate(blocks):
                    k.dma(hB[b % 2], src[b])
                    for hf in range(2):
                        cs = slice(hf * 512, (hf + 1) * 512)
                        k.mm(py[hf], [(Nn[:, j, bi * 128:(bi + 1) * 128], W2[:, j, cs]) for j in range(8)])
                        k.tt(yb[b % 2][:, cs], py[hf], b2B[:, cs], ALU.add)
                pending = post_steps(ti)
            for f_ in pending:
                f_()

    def phase_mixA(self, L, src, TT=256):
        k, I = self.k, self.I
        i = L // 2
        NP = self.NP
        with ExitStack() as st:
            Wi = k.sb(st, "Wi", [128, 8, INW], BF16)
            with ExitStack() as ws:
                self.wstage(ws)
                self.load_w(Wi, I["mix_w_in"][i], 8, INW, self.gcols(L, 0))
            self.S.barrier()
            trif = k.sb(st, "trif", [128, 128], F32)
            onesf = k.sb(st, "onesf", [128, 128], F32)
            maskT = k.sb(st, "maskT", [128, 128], BF16)
            gbB = k.sb(st, "gbB", [128, 8], F32)
            uT = [k.sb(st, "uT%d" % x, [128, 8, TT], BF16) for x in range(2)]
            X = [k.sb(st, "X%d" % j, [128, 4 + TT], BF16) for j in range(8)]
            Dq = k.sb(st, "Dq", [128, 32, 128], BF16)
            qk = [k.sb(st, "qk%d" % x, [128, 8, TT], BF16) for x in range(2)]
            qsks = [k.sb(st, "qsks%d" % x, [128, 8, TT], BF16) for x in range(2)]
            Vp = [k.sb(st, "Vp%d" % x, [128, 4, 260], BF16) for x in range(4)]
            sigo = [k.sb(st, "sigo%d" % x, [128, D], BF16) for x in range(4)]
            vst = [k.sb(st, "vst%d" % x, [128, 512], BF16) for x in range(2)]
            mixed = [k.sb(st, "mixed%d" % x, [128, D], BF16) for x in range(2)]
            C = [k.sb(st, "C%d" % h, [128, 260], F32) for h in range(4)]
            Cb = [k.sb(st, "Cb%d" % h, [128, 260], BF16) for h in range(4)]
            AT = [k.sb(st, "AT%d" % x, [128, 128], BF16) for x in range(4)]
            kd = [k.sb(st, "kd%d" % x, [128, 128], BF16) for x in range(4)]
            Pf = [k.sb(st, "Pf%d" % x, [128, 256], F32) for x in range(4)]
            hA = [k.sb(st, "hA%d" % x, [128, D], F32) for x in range(2)]
            ub = [k.sb(st, "ub%d" % x, [128, D], BF16) for x in range(2)]
            smA = [k.sb(st, "smA%d" % x, [128, 8], F32) for x in range(2)]
            gs_ = [k.sb(st, "gs%d" % x, [128, 64], F32) for x in range(4)]
            hs_ = [k.sb(st, "hsm%d" % x, [128, 48], F32) for x in range(2)]
            pT = k.ps(st, "pT", [128, 8, 128], BF16)
            pK = k.ps(st, "pK", [128, 8, 128], BF16)
            pp = [k.ps(st, "pp%d" % x, [128, 512], F32) for x in range(2)]
            pS = k.ps(st, "pS", [128, 4, 128], F32)
            pG = k.ps(st, "pG", [128, 512], F32)
            PU = [k.ps(st, "PU%d" % x, [128, 512], F32) for x in range(2)]

            k.dma(trif, V(self.csrc, I["c_trif"]))
            k.dma(onesf, V(self.csrc, I["c_onesf"]))
            k.dma(maskT, V(self.csrc, I["c_maskT"]))
            self.bcast_row(gbB, 18 + i, 8)
            for j in range(8):
                k.memset(X[j][:, 0:3], 0.0, eng="pool")
                for t in range(4):
                    k.ts(Dq[:, j * 4 + t, :], self.identb, self.col("cw_%d" % i, j * 4 + t, j * 4 + t + 1), ALU.mult)
            for h in range(4):
                k.memset(C[h], 0.0, eng="pool")
                k.memset(Cb[h], 0.0, eng="pool")
            for x in range(4):
                k.memset(Vp[x][:, :, 256:257], 1.0, eng="pool")
            cw = "cw_%d" % i
            ppi = [0]

            def proj_fm(col0, tt, u):
                p = pp[ppi[0] % 2][:, 0:tt]
                ppi[0] += 1
                k.mm(p, [(Wi[:, kc, col0:col0 + 128], u[:, kc, 0:tt]) for kc in range(8)])
                return p

            def proj_tm(col0, n, u, bi):
                p = pp[ppi[0] % 2][:, 0:n]
                ppi[0] += 1
                k.mm(p, [(u[:, kc, bi * 128:(bi + 1) * 128], Wi[:, kc, col0:col0 + n]) for kc in range(8)])
                return p

            tl = self.tiles(TT // 128)
            pre_a, pre_b = self.make_pre(tl, src, hA, ub, smA, uT, pT, eng="dve")

            def gsl(b):
                g = gs_[b % 4]
                return tuple(g[:, a:a + w] for a, w in (
                    (0, 8), (8, 8), (16, 4), (20, 4), (24, 8), (36, 4), (40, 4), (44, 4), (48, 4),
                    (52, 4), (56, 4), (60, 4)))

            def front(ti, bi, b):
                blocks = tl[ti]
                tt = len(blocks) * 128
                t0 = blocks[0] * 128
                u = uT[ti % 2]
                q_ = qk[ti % 2]
                qs_ = qsks[ti % 2]
                G = []
                if True:
                    def fm(j):
                        if j < 8:
                            p = proj_fm(j * 128, tt, u)
                            k.copy(X[j][:, 3:3 + tt], p, eng="dve")
                        if j > 0:
                            jj = j - 1
                            c = pp[ppi[0] % 2][:, 0:tt]
                            ppi[0] += 1
                            k.mm(c, [(Dq[:, jj * 4 + t, :], X[jj][:, t:t + tt]) for t in range(4)])
                            k.act(q_[:, jj, 0:tt], c, AF.Silu, bias=self.col("cb_%d" % i, jj, jj + 1))
                            k.copy(X[jj][:, 0:3], X[jj][:, tt:tt + 3], eng="dve")

                if True:
                    def qsk(j):
                        p = proj_fm(3080 + j * 128, tt, u)
                        k.copy(qs_[:, j, 0:tt], p, eng="dve")
                        if j == 7:
                            k.dma(V(self.b_q[blocks[0]], self.d_qs[:, :, t0:t0 + tt].rearrange("h p t -> p h t")),
                                  qs_[:, 0:4, 0:tt], q="pool")
                            k.dma(V(self.b_q[blocks[0]], self.d_ks[:, :, t0:t0 + tt].rearrange("h p t -> p h t")),
                                  qs_[:, 4:8, 0:tt], q="pool")
                if bi == 0:
                    for j in range(9):
                        G.append(lambda j=j: fm(j))
                if bi == len(blocks) - 1:
                    for j in range(8):
                        G.append(lambda j=j: qsk(j))
                lastb = bi == len(blocks) - 1 and ti + 1 < len(tl)
                if lastb:
                    G.insert(0, lambda: pre_a(ti + 1))
                bs = slice(bi * 128, (bi + 1) * 128)
                vp = Vp[b % 4]
                so = sigo[b % 4]
                g1, g2, ex, sp, nbG, e1, c_, r_, t3, e3, d_, eg = gsl(b)

                def gates1():
                    pg_ = pp[ppi[0] % 2][:, 0:8]
                    ppi[0] += 1
                    k.mm(pg_, [(u[:, kc, bs], Wi[:, kc, 3072:3080]) for kc in range(8)])
                    k.tt(g1, pg_, gbB, ALU.add)
                    k.act(g2, g1, AF.Tanh, scale=1.0 / 15.0)

                def gates1b():
                    k.act(ex, g2[:, 4:8], AF.Exp, scale=-15.0)
                    k.act(sp, ex, AF.Ln, bias=self.onec)
                    if b == 0:
                        k.ts(sp, sp, self.valid, ALU.mult)

                def vproj(hf):
                    p = proj_tm(1024 + hf * 512, 512, u, bi)
                    k.copy(vp[:, 2 * hf:2 * hf + 2, 0:256], p.re("p (a c) -> p a c", a=2), eng="dve")

                def gates2():
                    pc_ = pp[ppi[0] % 2]
                    ppi[0] += 1
                    k.mm(pc_[:, 0:4], [(trif, sp)])
                    k.mm(pc_[:, 4:8], [(onesf, sp)])
                    k.copy(nbG, pc_[:, 0:8], eng="dve")
                    k.stt(e1, g2[:, 0:4], 15.0, nbG[:, 0:4], ALU.mult, ALU.add)
                    k.act(c_, e1, AF.Exp)
                    k.act(r_, nbG[:, 0:4], AF.Exp, scale=-1.0, bias=self.lnq)
                    k.tt(t3, nbG[:, 0:4], nbG[:, 4:8], ALU.subtract)
                    k.stt(e3, g2[:, 0:4], 15.0, t3, ALU.mult, ALU.add)
                    k.act(d_, e3, AF.Exp)
                    k.act(eg, nbG[:, 4:8], AF.Exp, scale=-1.0)
                    if b == 0:
                        k.ts(c_, c_, self.valid, ALU.mult)
                        k.ts(d_, d_, self.valid, ALU.mult)

                def oproj(hf):
                    p = proj_tm(2048 + hf * 512, 512, u, bi)
                    k.act(so[:, hf * 512:(hf + 1) * 512], p, AF.Tanh, scale=0.5)

                def vsproj():
                    p = proj_tm(4104, 512, u, bi)
                    k.copy(vst[b % 2], p, eng="dve")
                    k.dma(V(self.b_q[b], self.d_vs[b * 128:(b + 1) * 128, :]), vst[b % 2], q="pool")

                G += [gates1, lambda: oproj(0), lambda: oproj(1), gates1b, lambda: vproj(0), lambda: vproj(1), gates2,
                      vsproj]
                if lastb:
                    G.append(lambda: pre_b(ti + 1))
                return G

            def heads(ti, bi, b):
                q_ = qk[ti % 2]
                bs = slice(bi * 128, (bi + 1) * 128)
                vp = Vp[b % 4]
                so = sigo[b % 4]
                g1, g2, ex, sp, nbG, e1, c_, r_, t3, e3, d_, eg = gsl(b)
                mx = mixed[b % 2]
                hs = hs_[b % 2]
                dd, ad, rec, fac, ssq, f2, msq, t4, rs, fs = (hs[:, 4 * a:4 * a + 4] for a in range(10))
                qT = [q_[:, h, bs] for h in range(4)]
                kT = [q_[:, 4 + h, bs] for h in range(4)]

                def s1():
                    for h in range(4):
                        k.mm(pS[:, h, :], [(kT[h], qT[h])])
                    k.transposes([(pK[:, h, :], kT[h]) for h in range(4)], self.identb)

                def s2():
                    for h in range(4):
                        k.stt(AT[h], pS[:, h, :], c_[:, h:h + 1], maskT, ALU.mult, ALU.mult)
                    for h in range(4):
                        k.act(kd[h], pK[:, h, :], AF.Copy, scale=d_[:, h:h + 1])

                def step3(h):
                    pu_ = PU[h % 2]
                    k.mm(pu_[:, 0:256], [(AT[h], vp[:, h, 0:256]), (qT[h], Cb[h][:, 0:256])])
                    k.mm(pu_[:, 256:512], [(kd[h], vp[:, h, 0:256])])
                    k.mm(pG[:, 16 + h:17 + h], [(AT[h], vp[:, h, 256:257]), (qT[h], Cb[h][:, 256:257])])
                    k.mm(pG[:, 20 + h:21 + h], [(kd[h], vp[:, h, 256:257])])

                def step4(h):
                    pu_ = PU[h % 2]
                    k.stt(C[h][:, 0:256], C[h][:, 0:256], eg[:, h:h + 1], pu_[:, 256:512], ALU.mult, ALU.add)
                    k.stt(C[h][:, 256:257], C[h][:, 256:257], eg[:, h:h + 1], pG[:, 20 + h:21 + h],
                          ALU.mult, ALU.add)
                    k.copy(Pf[h], pu_[:, 0:256], eng="dve")
                    k.copy(Cb[h][:, 0:257], C[h][:, 0:257], eng="dve")
                    k.act(self.junk[:, h * 256:(h + 1) * 256], Pf[h], AF.Square, accum=ssq[:, h:h + 1])

                def s7():
                    k.tt(dd, pG[:, 16:20], r_, ALU.mult)
                    k.act(ad, dd, AF.Abs)
                    k.ts(ad, ad, 1.0, ALU.max)
                    k.recip(rec, ad)
                    k.tt(fac, r_, rec, ALU.mult)
                    k.tt(f2, fac, fac, ALU.mult)
                    k.stt(msq, ssq, 1.0 / 256.0, f2, ALU.mult, ALU.mult)
                    self.rstd(rs, msq, 1.0, t4)
                    k.stt(fs, fac, 0.5, rs, ALU.mult, ALU.mult)

                def s8():
                    for h in range(4):
                        k.ts(Pf[h], Pf[h], fs[:, h:h + 1], ALU.mult)
                        k.stt(mx[:, h * 256:(h + 1) * 256], so[:, h * 256:(h + 1) * 256], 1.0, Pf[h],
                              ALU.add, ALU.mult)
                    k.dma(V(self.b_mm[b], self.d_mm[b * 128:(b + 1) * 128, :]), mx, q="pool")

                return [s1, s2, lambda: (step3(0), step3(1)), lambda: (step4(0), step4(1)),
                        lambda: (step3(2), step3(3)), lambda: (step4(2), step4(3)), s7, s8]

            order = [(ti, bi, b) for ti, blocks in enumerate(tl) for bi, b in enumerate(blocks)]
            pre_a(0)
            pre_b(0)
            for g in front(*order[0]):
                g()
            if len(order) > 1:
                for g in front(*order[1]):
                    g()
            for x, cur in enumerate(order):
                Fn = front(*order[x + 2]) if x + 2 < len(order) else []
                Hs = heads(*cur)
                nF, nH = len(Fn), len(Hs)
                fi = 0
                for hi, step in enumerate(Hs):
                    upto = (nF * (hi + 1)) // nH
                    while fi < upto:
                        Fn[fi]()
                        fi += 1
                    step()
                while fi < nF:
                    Fn[fi]()
                    fi += 1

    def phase_mixB(self, L, TQ=512):
        k, I = self.k, self.I
        NB, NP = self.NB, self.NP
        with ExitStack() as st:
            KT = k.sb(st, "KT", [128, 4, NP], BF16)
            Vt = k.sb(st, "Vt", [128, NB, 512], BF16)
            trib = k.sb(st, "trib", [128, 128], BF16)
            onesb = k.sb(st, "onesb", [128, 128], BF16)
            maskJ = k.sb(st, "maskJ", [128, 4, 512], BF16)
            qt = [k.sb(st, "qt%d" % x, [128, TQ], BF16) for x in range(2)]
            qn = [k.sb(st, "qn%d" % x, [128, TQ], BF16) for x in range(2)]
            ef = [k.sb(st, "ef%d" % x, [128, TQ], F32) for x in range(2)]
            spb = [k.sb(st, "spb%d" % x, [128, TQ], BF16) for x in range(3)]
            Rr = [k.sb(st, "Rr%d" % x, [128, TQ], F32) for x in range(2)]
            tX = [k.sb(st, "tX%d" % x, [128, TQ], F32) for x in range(3)]
            wT = [k.sb(st, "wT%d" % x, [128, TQ], BF16) for x in range(3)]
            ost = [k.sb(st, "ost%d" % x, [128, TQ], BF16) for x in range(2)]
            pz = [k.ps(st, "pz%d" % x, [128, 512], F32) for x in range(2)]
            pX = [k.ps(st, "pX%d" % x, [128, 512], F32) for x in range(2)]
            pc = [k.ps(st, "pc%d" % x, [128, 512], F32) for x in range(2)]
            pO = [k.ps(st, "pO%d" % x, [128, 512], F32) for x in range(2)]

            k.dma(trib, V(self.csrc, I["c_trib"]))
            k.dma(onesb, V(self.csrc, I["c_onesb"]))
            k.dma(maskJ, V(self.csrc, I["c_maskJ"].rearrange("p (j t) -> p j t", j=4)))
            allq = [V(b, None) for b in self.b_q]
            KTh = [V(Buf("KT%d" % h), KT.ap[:, h, :]) for h in range(4)]
            for h in range(4):
                k.S.op("sp", (lambda e, h=h: e.dma_start(out=KT.ap[:, h, :], in_=self.d_ks[h])), allq, [KTh[h]],
                       dma=True)
            Vth = []
            nq = 4
            for x in range(nq):
                n0, n1 = (NB * x) // nq, (NB * (x + 1)) // nq
                vb = V(Buf("Vt%d" % x), None)
                Vth.append((n0, n1, vb))
                if n1 > n0:
                    k.S.op("sp", (lambda e, n0=n0, n1=n1: e.dma_start(
                        out=Vt.ap[:, n0:n1, :],
                        in_=self.d_vs[n0 * 128:n1 * 128, :].rearrange("(n p) c -> p n c", p=128))),
                        allq, [vb], dma=True)

            def vbuf(n):
                for n0, n1, vb in Vth:
                    if n0 <= n < n1:
                        return vb

            units = []
            qi = 0
            qtiles = [[0]] + [list(range(t, min(NB, t + TQ // 128))) for t in range(1, NB, TQ // 128)]
            for ti, blocks in enumerate(qtiles):
                for h in range(4):
                    nmax = blocks[-1]
                    for n in range(nmax, -1, -1):
                        units.append(dict(ti=ti, blocks=blocks, h=h, n=n, first=(n == nmax), last=(n == 0), qi=qi))
                    qi += 1

            def common(u, ui):
                blocks, h, n = u["blocks"], u["h"], u["n"]
                nc_ = len(blocks) * 128
                c0 = max(0, n - blocks[0]) * 128
                kb = V(KTh[h].buf, KT.ap[:, h, n * 128:(n + 1) * 128])
                return blocks, h, n, c0, nc_, blocks[0] * 128, qt[u["qi"] % 2], qn[u["qi"] % 2], kb

            def stageA1(u, ui):
                blocks, h, n, c0, nc_, t0, q, qm, kb = common(u, ui)
                if u["first"]:
                    k.S.op("sp", (lambda e, o_=q.ap[:, 0:nc_], i_=self.d_qs[h, :, t0:t0 + nc_]:
                                  e.dma_start(out=o_, in_=i_)),
                           [V(self.b_q[bb], None) for bb in blocks], [q], dma=True)
                    k.ts(qm[:, 0:nc_], q[:, 0:nc_], -SB_SCALE, ALU.mult, eng="dve")
                z = pz[ui % 2][:, c0:nc_]
                k.mm(z, [(kb, q[:, c0:nc_])])
                k.act(z, z, AF.Exp, scale=SB_SCALE)

            def stageA2(u, ui):
                blocks, h, n, c0, nc_, t0, q, qm, kb = common(u, ui)
                s_ = spb[ui % 3][:, c0:nc_]
                k.act(s_, pz[ui % 2][:, c0:nc_], AF.Ln, bias=self.onec)
                if n >= blocks[0]:
                    k.tt(s_, s_, maskJ[:, n - blocks[0], c0:nc_], ALU.mult)
                if n == 0:
                    k.ts(s_, s_, self.valid, ALU.mult)

            def stageB1(u, ui):
                blocks, h, n, c0, nc_, t0, q, qm, kb = common(u, ui)
                first, last = u["first"], u["last"]
                s_ = spb[ui % 3][:, c0:nc_]
                Xp = pX[ui % 2][:, c0:nc_]
                cp = pc[ui % 2][:, c0:nc_]
                ci = blocks[-1] - n
                Rcur = Rr[ci % 2]
                Rnew = Rr[(ci + 1) % 2]
                k.mm(Xp, [(trib, s_), (kb, qm[:, c0:nc_])])
                if not last:
                    k.mm(cp, [(onesb, s_)])
                if first:
                    if not last:
                        k.copy(Rnew[:, c0:nc_], cp, eng="dve")
                else:
                    k.tt(Xp, Xp, Rcur[:, c0:nc_], ALU.add)
                    if not last:
                        k.tt(Rnew[:, c0:nc_], cp, Rcur[:, c0:nc_], ALU.add)
                if c0 > 0 and not last:
                    k.memset(Rnew[:, 0:c0], 0.0, eng="dve")

            def stageB2(u, ui):
                blocks, h, n, c0, nc_, t0, q, qm, kb = common(u, ui)
                w_ = wT[ui % 3][:, c0:nc_]
                k.act(w_, pX[ui % 2][:, c0:nc_], AF.Exp, scale=-1.0)
                if n >= blocks[0]:
                    k.tt(w_, w_, maskJ[:, n - blocks[0], c0:nc_], ALU.mult)

            def stageC(u, ui):
                blocks, h, n, c0, nc_, t0, q, qm, kb = common(u, ui)
                w_ = wT[ui % 3][:, c0:nc_]
                O = pO[u["qi"] % 2]
                vt = V(vbuf(n).buf, Vt.ap[:, n, h * 128:(h + 1) * 128])
                k.mm(O[:, c0:nc_], [(vt, w_)], start=u["first"], stop=u["last"])
                if u["last"]:
                    o_ = ost[u["qi"] % 2]
                    k.copy(o_[:, 0:nc_], O[:, 0:nc_], eng="dve")
                    k.dma(V(self.b_hs[blocks[0]], self.d_hs[h, :, t0:t0 + nc_]), o_[:, 0:nc_], q="pool")

            nu = len(units)
            for i in range(nu + 3):
                if i < nu:
                    stageA1(units[i], i)
                if 0 <= i - 2 < nu:
                    stageB2(units[i - 2], i - 2)
                if i < nu:
                    stageA2(units[i], i)
                if 0 <= i - 1 < nu:
                    stageB1(units[i - 1], i - 1)
                if 0 <= i - 3 < nu:
                    stageC(units[i - 3], i - 3)

    def phase_mixC(self, L, src, dst):
        k, I = self.k, self.I
        i = L // 2
        NB = self.NB
        with ExitStack() as st:
            Wo = k.sb(st, "Wo", [128, 12, D], BF16)
            gB = k.sb(st, "gB", [128, D], F32)
            self.bcast_row(gB, L * 4 + 1)
            hng = [self.col("hng_%d" % i, c, c + 1) for c in range(8)] + [None] * 4
            with ExitStack() as ws:
                self.wstage(ws)
                self.load_w(Wo, I["mix_w_out"][i], 12, D, hng)
            self.S.barrier()
            mmi = [k.sb(st, "mmi%d" % x, [128, D], BF16) for x in range(2)]
            mmT = [k.sb(st, "mmT%d" % x, [128, 8, 128], BF16) for x in range(2)]
            hsT = [k.sb(st, "hsT%d" % x, [128, 4, 128], BF16) for x in range(2)]
            hB = [k.sb(st, "hB%d" % x, [128, D], F32) for x in range(2)]
            tmp = k.sb(st, "tmp", [128, D], F32)
            smB = [k.sb(st, "smB%d" % x, [128, 8], F32) for x in range(2)]
            pT = [k.ps(st, "pT%d" % x, [128, 8, 128], BF16) for x in range(2)]
            py = [k.ps(st, "py%d" % x, [128, 512], F32) for x in range(4)]
            tmp2 = [tmp, k.sb(st, "tmpb", [128, D], F32)]

            def post_steps(b):
                x = b % 2
                h, sm, tm = hB[x], smB[x], tmp2[x]
                yy = [py[2 * x], py[2 * x + 1]]
                ss0, ss1, ss, t1, rs = (sm[:, c:c + 1] for c in range(5))

                def q1():
                    k.act(self.junk[:, 0:512], yy[0], AF.Square, accum=ss0)
                    k.act(self.junk[:, 512:1024], yy[1], AF.Square, accum=ss1)

                def q2():
                    k.tt(ss, ss0, ss1, ALU.add)
                    self.rstd(rs, ss, 1.0 / D, t1)
                    if b == 0:
                        k.tt(rs, rs, self.valid, ALU.mult)

                def q3():
                    for hf in range(2):
                        cs = slice(hf * 512, (hf + 1) * 512)
                        k.stt(tm[:, cs], yy[hf], rs, gB[:, cs], ALU.mult, ALU.mult)

                def q4():
                    k.tt(h, h, tm, ALU.add)
                    if dst[b] is not None:
                        k.dma(dst[b], h, q="pool")
                return [q1, q2, q3, q4]

            pend = [lambda: None] * 4
            for b in range(NB):
                x = b % 2
                k.dma(mmi[x], V(self.b_mm[b], self.d_mm[b * 128:(b + 1) * 128, :]))
                k.S.op("sp", (lambda e, x=x, b=b: e.dma_start(
                    out=hsT[x].ap, in_=self.d_hs[:, :, b * 128:(b + 1) * 128].rearrange("h p t -> p h t"))),
                    [V(self.b_hs[self.hs_first[b]], None)], [hsT[x]], dma=True)
                k.dma(hB[x], src[b])
                k.transposes([(pT[x][:, c, :], mmi[x][:, c * 128:(c + 1) * 128]) for c in range(8)], self.identb)
                pend[0]()
                k.copy(mmT[x], pT[x], eng="act")
                pend[1]()
                yy = [py[2 * x], py[2 * x + 1]]
                for hf in range(2):
                    cs = slice(hf * 512, (hf + 1) * 512)
                    k.mm(yy[hf], [(mmT[x][:, c, :], Wo[:, c, cs]) for c in range(8)] +
                         [(hsT[x][:, hh, :], Wo[:, 8 + hh, cs]) for hh in range(4)])
                    pend[2 + hf]()
                pend = post_steps(b)
            for f_ in pend:
                f_()


def colT(v):
    v = np.asarray(v, np.float32)
    return np.ascontiguousarray(v.reshape(-1, 128).T)


def host_consts():
    bf = ml_dtypes.bfloat16
    c = {}
    r = np.arange(128)
    c["c_identb"] = np.eye(128, dtype=np.float32).astype(bf)
    c["c_onesb"] = np.ones((128, 128), np.float32).astype(bf)
    c["c_trib"] = (r[:, None] >= r[None, :]).astype(np.float32).astype(bf)
    c["c_maskT"] = (r[:, None] <= r[None, :]).astype(np.float32).astype(bf)
    t = np.arange(512)
    mj = np.stack([((128 * j + r[:, None]) < t[None, :]).astype(np.float32) for j in range(4)], axis=1)
    c["c_maskJ"] = np.ascontiguousarray(mj.reshape(128, 4 * 512)).astype(bf)
    c["c_trif"] = (r[:, None] <= r[None, :]).astype(np.float32)
    c["c_onesf"] = np.ones((128, 128), np.float32)
    return c


def prep_shared(inp):
    sh = dict(host_consts())
    f = lambda n: np.asarray(inp[n], np.float32)
    ng = f("norm_g")
    cols = np.zeros((128, NCOLS), np.float32)

    def put(name, arr):
        o, w = COLS[name]
        assert arr.shape == (128, w), (name, arr.shape, w)
        cols[:, o:o + w] = arr

    put("ng", np.concatenate([colT(ng[l, j]) for l in range(4) for j in range(4)], axis=1))
    put("valid", (np.arange(128) >= 112).astype(np.float32)[:, None])
    for i in range(2):
        put("b1_%d" % i, colT(f("conv_b_pw1")[i]))
        wdw = f("conv_w_dw")[i]
        put("wdw_%d" % i, np.ascontiguousarray(wdw.reshape(31, 8, 128).transpose(2, 1, 0)).reshape(128, 248))
        put("bdw_%d" % i, colT(f("conv_b_dw")[i]))
        put("lng_%d" % i, colT(f("conv_ln_g")[i]))
        put("lnb_%d" % i, colT(f("conv_ln_b")[i]))
        cw = f("mix_qk_conv_w")[i]
        put("cw_%d" % i, np.ascontiguousarray(cw.reshape(4, 8, 128).transpose(2, 1, 0)).reshape(128, 32))
        put("cb_%d" % i, colT(f("mix_qk_conv_b")[i]))
        put("hng_%d" % i, colT(f("mix_hnorm_g")[i]))
    sh["colsT"] = cols
    rows = np.zeros((NROWS, D), np.float32)
    rows[0:16] = ng.reshape(16, D)
    rows[16:18] = f("conv_b_pw2")
    rows[18:20, 0:8] = f("mix_gate_b")
    sh["rows"] = rows
    for n in ("ffn_w_gate", "ffn_w_up", "ffn_w_down", "conv_w_pw1", "conv_w_pw2", "mix_w_in", "mix_w_out"):
        sh[n] = np.ascontiguousarray(f(n))
    return sh


def make_hp(x_b, meta, NB):
    hp = np.zeros((NB * 128, D), np.float32)
    hp[112:128] = meta
    hp[128:] = x_b
    return hp


ALL_LAYERS = [("M", 0), ("F", 0), ("V", 1), ("F", 1), ("M", 2), ("F", 2), ("V", 3), ("F", 3)]


def run(inp, NB, layers, n_cores=8, trace=False, debug=False):
    x = np.asarray(inp["x"], np.float32)
    meta = np.asarray(inp["meta"], np.float32)
    prog = Prog(NB, layers, debug=debug)
    nc = prog.build()
    sh = prep_shared(inp)
    in_maps = []
    for c in range(n_cores):
        m = dict(sh)
        m["hp"] = make_hp(x[c], meta, NB)
        in_maps.append(m)
    res = run_bass_kernel_spmd(nc, in_maps, core_ids=list(range(n_cores)), trace=trace)
    out = np.stack([np.asarray(r["out"], np.float32) for r in res.results], axis=0)
    return out, res


def kernel(**inputs):
    out, _ = run(inputs, 65, ALL_LAYERS)
    return out
```

```python
import numpy as np
import ml_dtypes
from contextlib import ExitStack
import concourse.bass as bass
import concourse.mybir as mybir
from concourse.bass_utils import run_bass_kernel_spmd

F32 = mybir.dt.float32
BF16 = mybir.dt.bfloat16
AF = mybir.ActivationFunctionType
ALU = mybir.AluOpType

D = 1024
FF = 2816
NFC = FF // 128
EPS = 1e-6
NDMA = 40
INW = 4616


class Buf:
    __slots__ = ("name", "excl", "last_w", "readers")

    def __init__(self, name, excl=False):
        self.name = name
        self.excl = excl
        self.last_w = None
        self.readers = {}


class V:
    __slots__ = ("buf", "ap")

    def __init__(self, buf, ap):
        self.buf = buf
        self.ap = ap

    def __getitem__(self, k):
        return V(self.buf, self.ap[k])

    def re(self, pat, **kw):
        return V(self.buf, self.ap.rearrange(pat, **kw))


class Op:
    __slots__ = ("eng", "fn", "deps", "dma", "slot", "ninc", "tok", "idx")

    def __init__(self, eng, fn, dma):
        self.eng = eng
        self.fn = fn
        self.dma = dma
        self.deps = ()
        self.slot = -1
        self.ninc = False
        self.tok = None


ENGS = ("pe", "act", "dve", "pool", "sp")


class Sched:
    def __init__(self, nc):
        self.nc = nc
        self.ops = {e: [] for e in ENGS}
        self.pending = {e: set() for e in ENGS}
        self.dma_last = [None] * NDMA
        self.rr = 0
        self.last = {e: None for e in ENGS}
        self.nall = 0

    def op(self, eng, fn, reads=(), writes=(), dma=False):
        o = Op(eng, fn, dma)
        o.idx = self.nall
        self.nall += 1
        deps = {}

        def add(d):
            if d is None or d is o:
                return
            if d.dma:
                deps[("d", d.idx)] = d
            else:
                if d.eng == "pe" and eng == "pe" and not dma:
                    return
                k = ("c", d.eng)
                if k not in deps or deps[k].idx < d.idx:
                    deps[k] = d

        rb, wb = [], []
        for v in reads:
            b = v.buf if isinstance(v, V) else v
            (wb if b.excl else rb).append(b)
        for v in writes:
            b = v.buf if isinstance(v, V) else v
            wb.append(b)
        for b in rb:
            add(b.last_w)
        for b in wb:
            add(b.last_w)
            for r in b.readers.values():
                add(r)
        for d in self.pending[eng]:
            add(d)
        self.pending[eng] = set()
        if dma:
            o.slot = self.rr
            self.rr = (self.rr + 1) % NDMA
            add(self.dma_last[o.slot])
            self.dma_last[o.slot] = o
        o.deps = tuple(deps.values())
        for b in rb:
            key = ("d", o.idx) if dma else eng
            b.readers[key] = o
        for b in wb:
            b.last_w = o
            b.readers = {}
        self.ops[eng].append(o)
        if not dma:
            self.last[eng] = o
        return o

    def barrier(self):
        deps_r = []
        o = Op("sp", lambda e: e.drain(), False)
        o.idx = self.nall
        self.nall += 1
        deps = {}
        for e in ENGS:
            if self.last[e] is not None:
                deps[("c", e)] = self.last[e]
        for d in self.dma_last:
            if d is not None:
                deps[("d", d.idx)] = d
        for d in self.pending["sp"]:
            deps[("x", d.idx)] = d
        o.deps = tuple(deps.values())
        self.ops["sp"].append(o)
        self.last["sp"] = o
        for e in ENGS:
            self.pending[e] = {o}
        return o

    def emit(self, stack):
        nc = self.nc
        for e in ENGS:
            for o in self.ops[e]:
                for d in o.deps:
                    d.ninc = True
        esem = {e: stack.enter_context(nc.semaphore("s_" + e)) for e in ENGS}
        dsem = [stack.enter_context(nc.semaphore("d_%d" % i)) for i in range(NDMA)]
        dcnt = [0] * NDMA
        allops = []
        for e in ENGS:
            allops.extend(self.ops[e])
        allops.sort(key=lambda o: o.idx)
        ecnt = {e: 0 for e in ENGS}
        for o in allops:
            if o.dma:
                dcnt[o.slot] += 16
                o.tok = (dsem[o.slot], dcnt[o.slot], ("d", o.slot))
            elif o.ninc:
                ecnt[o.eng] += 1
                o.tok = (esem[o.eng], ecnt[o.eng], ("e", o.eng))
        handles = {"pe": "tensor", "act": "scalar", "dve": "vector", "pool": "gpsimd", "sp": "sync"}
        block = stack.enter_context(nc.Block())

        def run(eng_name):
            def body(engine):
                seen = {}
                for o in self.ops[eng_name]:
                    waits = {}
                    for d in o.deps:
                        sem, val, key = d.tok
                        if key not in waits or waits[key][1] < val:
                            waits[key] = (sem, val)
                    for key, (sem, val) in waits.items():
                        if seen.get(key, 0) < val:
                            engine.wait_ge(sem, val)
                            seen[key] = val
                    ins = o.fn(engine)
                    if o.tok is not None:
                        ins.then_inc(o.tok[0], 16 if o.dma else 1)
            return body

        block.tensor(run("pe"))
        block.scalar(run("act"))
        block.vector(run("dve"))
        block.gpsimd(run("pool"))
        block.sync(run("sp"))


def _ap(x):
    return x.ap if isinstance(x, V) else x


class K:
    def __init__(self, nc, S):
        self.nc = nc
        self.S = S
        self.n = 0

    def name(self, p):
        self.n += 1
        return "%s_%d" % (p, self.n)

    def sb(self, stack, name, shape, dt):
        t = stack.enter_context(self.nc.sbuf_tensor(self.name(name), list(shape), dt))
        return V(Buf(name), t[:] if len(shape) == 2 else t[:])

    def ps(self, stack, name, shape, dt):
        t = stack.enter_context(self.nc.psum_tensor(self.name(name), list(shape), dt))
        return V(Buf(name, excl=True), t[:])

    def dma(self, out, in_, q="sp", **kw):
        o_, i_ = out.ap, in_.ap
        return self.S.op(q, lambda e: e.dma_start(out=o_, in_=i_, **kw), [in_], [out], dma=True)

    def mm(self, out, pairs, start=True, stop=True, extra_reads=()):
        o_ = out.ap
        pr = [(a.ap, b.ap) for a, b in pairs]
        n = len(pr)

        def fn(e):
            ins = None
            for i, (a, b) in enumerate(pr):
                ins = e.matmul(o_, lhsT=a, rhs=b, start=(start and i == 0), stop=(stop and i == n - 1))
            return ins

        reads = [x for p in pairs for x in p] + list(extra_reads)
        return self.S.op("pe", fn, reads, [out])

    def transposes(self, outs_ins, ident):
        pr = [(o.ap, i.ap) for o, i in outs_ins]
        id_ = ident.ap

        def fn(e):
            ins = None
            for o, i in pr:
                ins = e.transpose(o, i, id_)
            return ins

        return self.S.op("pe", fn, [i for _, i in outs_ins] + [ident], [o for o, _ in outs_ins])

    def act(self, out, in_, func, bias=None, scale=None, accum=None):
        kw = {}
        reads = [in_]
        writes = [out]
        if bias is not None:
            kw["bias"] = _ap(bias)
            if isinstance(bias, V):
                reads.append(bias)
        if scale is not None:
            kw["scale"] = _ap(scale)
            if isinstance(scale, V):
                reads.append(scale)
        if accum is not None:
            kw["accum_out"] = accum.ap
            writes.append(accum)
        o_, i_ = out.ap, in_.ap
        return self.S.op("act", lambda e: e.activation(out=o_, in_=i_, func=func, **kw), reads, writes)

    def tt(self, out, a, b, op, eng="dve"):
        o_, a_, b_ = out.ap, a.ap, b.ap
        return self.S.op(eng, lambda e: e.tensor_tensor(out=o_, in0=a_, in1=b_, op=op), [a, b], [out])

    def ts(self, out, a, s1, op0, s2=None, op1=None, eng="dve", accum=None):
        reads = [a]
        for s in (s1, s2):
            if isinstance(s, V):
                reads.append(s)
        o_, a_ = out.ap, a.ap
        s1_, s2_ = _ap(s1), _ap(s2)
        kw = {}
        writes = [out]
        if op1 is not None:
            kw["op1"] = op1
        if accum is not None:
            kw["accum_out"] = accum.ap
            writes.append(accum)
        return self.S.op(eng, lambda e: e.tensor_scalar(out=o_, in0=a_, scalar1=s1_, scalar2=s2_, op0=op0, **kw),
                         reads, writes)

    def stt(self, out, a, s, b, op0, op1):
        reads = [a, b]
        if isinstance(s, V):
            reads.append(s)
        o_, a_, b_, s_ = out.ap, a.ap, b.ap, _ap(s)
        return self.S.op("dve", lambda e: e.scalar_tensor_tensor(out=o_, in0=a_, scalar=s_, in1=b_, op0=op0, op1=op1),
                         reads, [out])

    def copy(self, out, in_, eng="dve"):
        o_, i_ = out.ap, in_.ap
        if eng == "act":
            return self.S.op("act", lambda e: e.activation(out=o_, in_=i_, func=AF.Copy), [in_], [out])
        return self.S.op(eng, lambda e: e.tensor_copy(out=o_, in_=i_), [in_], [out])

    def memset(self, out, val, eng="dve"):
        o_ = out.ap
        return self.S.op(eng, lambda e: e.memset(o_, val), [], [out])

    def recip(self, out, in_):
        o_, i_ = out.ap, in_.ap
        return self.S.op("dve", lambda e: e.reciprocal(out=o_, in_=i_), [in_], [out])


def _mk_cols():
    off = {}
    n = 0

    def add(name, w):
        nonlocal n
        off[name] = (n, w)
        n += w

    add("ng", 16 * 8)
    add("valid", 1)
    for i in range(2):
        add("b1_%d" % i, 16)
        add("wdw_%d" % i, 8 * 31)
        add("bdw_%d" % i, 8)
        add("lng_%d" % i, 8)
        add("lnb_%d" % i, 8)
        add("cw_%d" % i, 32)
        add("cb_%d" % i, 8)
        add("hng_%d" % i, 8)
    return off, n


COLS, NCOLS = _mk_cols()
NROWS = 20
SB_SCALE = 128 ** -0.5
LN_QSCALE = float(np.log(128 ** -0.5))


class Prog:
    def __init__(self, NB, layers, debug=False):
        self.NB = NB
        self.NP = NB * 128
        self.layers = layers
        self.debug = debug
        nc = bass.Bass("TRN2", target_bir_lowering=False)
        self.nc = nc
        self.S = Sched(nc)
        self.k = K(nc, self.S)
        self.cast_rr = 0

    def din(self, name, shape, dt=F32):
        return self.nc.dram_tensor(name, list(shape), dt, kind="ExternalInput").ap()

    def dscr(self, name, shape, dt, out=False):
        kind = "ExternalOutput" if (out or self.debug) else "Internal"
        return self.nc.dram_tensor(name, list(shape), dt, kind=kind).ap()

    def col(self, name, a=0, b=None):
        o, w = COLS[name]
        if b is None:
            b = w
        return self.colsT[:, o + a:o + b]

    def build(self):
        nc, k, S = self.nc, self.k, self.S
        NB, NP = self.NB, self.NP
        I = {}
        I["hp"] = self.din("hp", [NP, D])
        I["colsT"] = self.din("colsT", [128, NCOLS])
        I["rows"] = self.din("rows", [NROWS, D])
        I["ffn_w_gate"] = self.din("ffn_w_gate", [4, D, FF])
        I["ffn_w_up"] = self.din("ffn_w_up", [4, D, FF])
        I["ffn_w_down"] = self.din("ffn_w_down", [4, FF, D])
        I["conv_w_pw1"] = self.din("conv_w_pw1", [2, D, 2 * D])
        I["conv_w_pw2"] = self.din("conv_w_pw2", [2, D, D])
        I["mix_w_in"] = self.din("mix_w_in", [2, D, INW])
        I["mix_w_out"] = self.din("mix_w_out", [2, 1536, D])
        I["c_identb"] = self.din("c_identb", [128, 128], BF16)
        I["c_onesb"] = self.din("c_onesb", [128, 128], BF16)
        I["c_trib"] = self.din("c_trib", [128, 128], BF16)
        I["c_maskT"] = self.din("c_maskT", [128, 128], BF16)
        I["c_maskJ"] = self.din("c_maskJ", [128, 4 * 512], BF16)
        I["c_trif"] = self.din("c_trif", [128, 128], F32)
        I["c_onesf"] = self.din("c_onesf", [128, 128], F32)
        self.I = I
        out = self.dscr("out", [NP - 128, D], F32, out=True)
        hs = self.dscr("hs", [NP, D], F32)
        self.d_mm = self.dscr("s_mm", [NP, D], BF16)
        self.d_qs = self.dscr("s_qs", [4, 128, NP], BF16)
        self.d_ks = self.dscr("s_ks", [4, 128, NP], BF16)
        self.d_vs = self.dscr("s_vs", [NP, 512], BF16)
        self.d_hs = self.dscr("s_hs", [4, 128, NP], BF16)
        self.hp_blk = [V(Buf("hp%d" % b), I["hp"][b * 128:(b + 1) * 128, :]) for b in range(NB)]
        self.hs_blk = [V(Buf("hs%d" % b), hs[b * 128:(b + 1) * 128, :]) for b in range(NB)]
        self.out_blk = [None] + [V(Buf("out%d" % b), out[(b - 1) * 128:b * 128, :]) for b in range(1, NB)]
        self.b_mm = [Buf("mm%d" % b) for b in range(NB)]
        self.b_q = [Buf("q%d" % b) for b in range(NB)]
        self.b_hs = [Buf("hsT%d" % b) for b in range(NB)]
        self.hs_first = [0] + [((b - 1) // 4) * 4 + 1 for b in range(1, NB)]
        self.csrc = Buf("csrc")

        with ExitStack() as gs:
            self.colsT = k.sb(gs, "colsT", [128, NCOLS], F32)
            self.identb = k.sb(gs, "identb", [128, 128], BF16)
            self.epsc = k.sb(gs, "epsc", [128, 1], F32)
            self.junk = k.sb(gs, "junk", [128, D], BF16)
            k.dma(self.colsT, V(self.csrc, I["colsT"]))
            k.dma(self.identb, V(self.csrc, I["c_identb"]))
            k.memset(self.epsc, EPS)
            self.onec = k.sb(gs, "onec", [128, 1], F32)
            k.memset(self.onec, 1.0)
            self.lnq = k.sb(gs, "lnq", [128, 1], F32)
            k.memset(self.lnq, LN_QSCALE)
            self.valid = self.col("valid")
            cur_in = self.hp_blk
            nl = len(self.layers)
            for li, (kind, L) in enumerate(self.layers):
                last = li == nl - 1
                dst = self.out_blk if last else self.hs_blk
                if kind == "F":
                    self.phase_ffn(L, cur_in, dst)
                elif kind == "V":
                    self.phase_conf(L, cur_in, dst)
                elif kind == "M":
                    self.phase_mixA(L, cur_in)
                    S.barrier()
                    self.phase_mixB(L)
                    S.barrier()
                    self.phase_mixC(L, cur_in, dst)
                elif kind == "MA":
                    self.phase_mixA(L, cur_in)
                elif kind == "MB":
                    self.phase_mixB(L)
                elif kind == "MC":
                    self.phase_mixC(L, cur_in, dst)
                if kind != "MA" and kind != "MB":
                    cur_in = self.hs_blk
                S.barrier()
            S.barrier()
            S.emit(gs)
        return nc

    def wstage(self, st, piece=2048, n=6):
        self.stg = [self.k.sb(st, "stg%d" % i, [128, piece], F32) for i in range(n)]
        self.stg_i = 0
        self.piece = piece

    def load_w(self, dst, src, nk, ncols, gcol=None, c_lo=0):
        k = self.k
        piece = self.piece
        for kc in range(nk):
            for c0 in range(0, ncols, piece):
                c1 = min(ncols, c0 + piece)
                stg = self.stg[self.stg_i % len(self.stg)]
                self.stg_i += 1
                k.dma(stg[:, 0:c1 - c0], V(self.csrc, src[kc * 128:(kc + 1) * 128, c0:c1]),
                      q=("sp", "pool")[self.cast_rr % 2])
                eng = ("dve", "act")[(self.cast_rr // 2) % 2]
                self.cast_rr += 1
                o = dst[:, kc, c_lo + c0:c_lo + c1]
                i = stg[:, 0:c1 - c0]
                g = None if (gcol is None or gcol[kc] is None) else gcol[kc]
                if g is None:
                    k.copy(o, i, eng=eng)
                elif eng == "act":
                    k.act(o, i, AF.Copy, scale=g)
                else:
                    k.ts(o, i, g, ALU.mult, eng=eng)

    def gcols(self, L, j):
        o = (L * 4 + j) * 8
        return [self.col("ng", o + c, o + c + 1) for c in range(8)]

    def bcast_row(self, dst, r, n=D):
        self.k.dma(dst, V(self.csrc, self.I["rows"][r:r + 1, 0:n].partition_broadcast(128)))

    def rstd(self, out, ss, scale, tmp):
        k = self.k
        k.act(tmp, ss, AF.Ln, bias=self.epsc, scale=scale)
        k.act(out, tmp, AF.Exp, scale=-0.5)

    def prenorm_a(self, hblk, ub, sm):
        k = self.k
        ss, t1, rs = sm[:, 0:1], sm[:, 1:2], sm[:, 2:3]
        k.act(self.junk, hblk, AF.Square, accum=ss)
        self.rstd(rs, ss, 1.0 / D, t1)
        k.ts(ub, hblk, rs, ALU.mult)

    def prenorm_b(self, ub, pT, uT_dst, eng="act"):
        k = self.k
        k.transposes([(pT[:, c, :], ub[:, c * 128:(c + 1) * 128]) for c in range(8)], self.identb)
        k.copy(uT_dst, pT, eng=eng)

    def make_pre(self, tl, src, hA, ub, smA, uT, pT, eng="act"):
        k = self.k

        def pre_a(ti):
            for bi, b in enumerate(tl[ti]):
                k.dma(hA[bi % 2], src[b])
                self.prenorm_a(hA[bi % 2], ub[bi % 2], smA[bi % 2])

        def pre_b(ti):
            u = uT[ti % 2]
            for bi, b in enumerate(tl[ti]):
                self.prenorm_b(ub[bi % 2], pT, u[:, :, bi * 128:(bi + 1) * 128], eng=eng)

        return pre_a, pre_b

    def postnorm_res(self, ys, hblk, gB, sm, tmp, b):
        k = self.k
        ss0, ss1, ss, t1, rs = (sm[:, i:i + 1] for i in range(5))
        k.act(self.junk[:, 0:512], ys[0], AF.Square, accum=ss0)
        k.act(self.junk[:, 512:1024], ys[1], AF.Square, accum=ss1)
        k.tt(ss, ss0, ss1, ALU.add)
        self.rstd(rs, ss, 1.0 / D, t1)
        if b == 0:
            k.tt(rs, rs, self.valid, ALU.mult)
        for hf in range(2):
            k.stt(tmp[:, hf * 512:(hf + 1) * 512], ys[hf], rs, gB[:, hf * 512:(hf + 1) * 512], ALU.mult, ALU.mult)
        k.tt(hblk, hblk, tmp, ALU.add)

    def tiles(self, nbt):
        NB = self.NB
        return [list(range(t * nbt, min(NB, (t + 1) * nbt))) for t in range((NB + nbt - 1) // nbt)]

    def phase_ffn(self, L, src, dst, TT=256):
        k, I = self.k, self.I
        NSPLIT = 18
        with ExitStack() as st:
            Wg = k.sb(st, "Wg", [128, 8, FF], BF16)
            Wu = k.sb(st, "Wu", [128, 8, FF], BF16)
            Wd = k.sb(st, "Wd", [128, NFC, D], BF16)
            gB = k.sb(st, "gB", [128, D], F32)
            self.bcast_row(gB, L * 4 + 3)
            g2 = self.gcols(L, 2)
            with ExitStack() as ws:
                self.wstage(ws)
                self.load_w(Wg, I["ffn_w_gate"][L], 8, FF, g2)
                self.load_w(Wu, I["ffn_w_up"][L], 8, FF, g2)
                self.load_w(Wd, I["ffn_w_down"][L], NFC, D, None)
            self.S.barrier()
            uT = [k.sb(st, "uT%d" % i, [128, 8, TT], BF16) for i in range(2)]
            aT1 = k.sb(st, "aT1", [128, NSPLIT, TT], BF16)
            aT2 = k.sb(st, "aT2", [128, NFC - NSPLIT, TT], BF16)
            hA = [k.sb(st, "hA%d" % i, [128, D], F32) for i in range(2)]
            hB = [k.sb(st, "hB%d" % i, [128, D], F32) for i in range(2)]
            ub = [k.sb(st, "ub%d" % i, [128, D], BF16) for i in range(2)]
            sg = [k.sb(st, "sg%d" % i, [128, TT], F32) for i in range(2)]
            tmp = k.sb(st, "tmp", [128, D], F32)
            smA = [k.sb(st, "smA%d" % i, [128, 8], F32) for i in range(2)]
            smB = [k.sb(st, "smB%d" % i, [128, 8], F32) for i in range(2)]
            pT = k.ps(st, "pT", [128, 8, 128], BF16)
            pgu = [k.ps(st, "pgu%d" % i, [128, 512], F32) for i in range(3)]
            py = [k.ps(st, "py%d" % i, [128, 512], F32) for i in range(4)]

            def aT(f):
                return aT1[:, f] if f < NSPLIT else aT2[:, f - NSPLIT]

            tl = self.tiles(TT // 128)
            pre_a, pre_b = self.make_pre(tl, src, hA, ub, smA, uT, pT)
            pre_a(0)
            pre_b(0)
            for ti, blocks in enumerate(tl):
                tt = len(blocks) * 128
                u = uT[ti % 2]
                if ti + 1 < len(tl):
                    pre_a(ti + 1)
                for f in range(NFC):
                    g_ = pgu[f % 3][:, 0:tt]
                    u_ = pgu[f % 3][:, 256:256 + tt]
                    k.mm(g_, [(Wg[:, kc, f * 128:(f + 1) * 128], u[:, kc, 0:tt]) for kc in range(8)])
                    k.mm(u_, [(Wu[:, kc, f * 128:(f + 1) * 128], u[:, kc, 0:tt]) for kc in range(8)])
                    s_ = sg[f % 2][:, 0:tt]
                    k.act(s_, g_, AF.Silu)
                    k.tt(aT(f)[:, 0:tt], s_, u_, ALU.mult)
                if ti + 1 < len(tl):
                    pre_b(ti + 1)
                for bi, b in enumerate(blocks):
                    h = hB[b % 2]
                    k.dma(h, src[b])
                    yy = [py[2 * (b % 2)], py[2 * (b % 2) + 1]]
                    bs = slice(bi * 128, (bi + 1) * 128)
                    for hf in range(2):
                        cs = slice(hf * 512, (hf + 1) * 512)
                        k.mm(yy[hf], [(aT1[:, f, bs], Wd[:, f, cs]) for f in range(NSPLIT)], start=True, stop=False)
                    for hf in range(2):
                        cs = slice(hf * 512, (hf + 1) * 512)
                        k.mm(yy[hf], [(aT2[:, f - NSPLIT, bs], Wd[:, f, cs]) for f in range(NSPLIT, NFC)],
                             start=False, stop=True)
                    self.postnorm_res(yy, h, gB, smB[b % 2], tmp, b)
                    if dst[b] is not None:
                        k.dma(dst[b], h, q="pool")

    def phase_conf(self, L, src, dst, TT=256):
        k, I = self.k, self.I
        i = L // 2
        with ExitStack() as st:
            W1 = k.sb(st, "W1", [128, 8, 2 * D], BF16)
            W2 = k.sb(st, "W2", [128, 8, D], BF16)
            Dg = k.sb(st, "Dg", [128, 8 * 31, 128], BF16)
            gB = k.sb(st, "gB", [128, D], F32)
            b2B = k.sb(st, "b2B", [128, D], F32)
            onesb = k.sb(st, "onesb", [128, 128], BF16)
            with ExitStack() as ws:
                self.wstage(ws)
                self.load_w(W1, I["conv_w_pw1"][i], 8, 2 * D, self.gcols(L, 0))
                self.load_w(W2, I["conv_w_pw2"][i], 8, D, None)
            self.S.barrier()
            uT = [k.sb(st, "uT%d" % x, [128, 8, TT], BF16) for x in range(2)]
            Y = [k.sb(st, "Y%d" % j, [128, 30 + TT], BF16) for j in range(8)]
            Z = k.sb(st, "Z", [128, 8, TT], F32)
            zb = k.sb(st, "zb", [128, 8, TT], BF16)
            zq = k.sb(st, "zq", [128, 8, TT], BF16)
            Nn = k.sb(st, "Nn", [128, 8, TT], BF16)
            m_ = k.sb(st, "m_", [128, TT], F32)
            msq = k.sb(st, "msq", [128, TT], F32)
            var = k.sb(st, "var", [128, TT], F32)
            rs_ = k.sb(st, "rs_", [128, TT], F32)
            tz = [k.sb(st, "tz%d" % x, [128, TT], F32) for x in range(2)]
            sg = [k.sb(st, "sg%d" % x, [128, TT], F32) for x in range(2)]
            hA = [k.sb(st, "hA%d" % x, [128, D], F32) for x in range(2)]
            hB = [k.sb(st, "hB%d" % x, [128, D], F32) for x in range(2)]
            ub = [k.sb(st, "ub%d" % x, [128, D], BF16) for x in range(2)]
            yb = [k.sb(st, "yb%d" % x, [128, D], F32) for x in range(2)]
            tmp = [k.sb(st, "tmp%d" % x, [128, D], F32) for x in range(2)]
            smA = [k.sb(st, "smA%d" % x, [128, 8], F32) for x in range(2)]
            smB = [k.sb(st, "smB%d" % x, [128, 8], F32) for x in range(2)]
            pT = k.ps(st, "pT", [128, 8, 128], BF16)
            pag = [k.ps(st, "pag%d" % x, [128, 512], F32) for x in range(2)]
            pcv = [k.ps(st, "pcv%d" % x, [128, 512], F32) for x in range(2)]
            pss = k.ps(st, "pss", [128, 512], F32)
            py = [k.ps(st, "py%d" % x, [128, 512], F32) for x in range(2)]

            self.bcast_row(gB, L * 4 + 1)
            self.bcast_row(b2B, 16 + i)
            b1h = k.sb(st, "b1h", [128, 8], F32)
            k.ts(b1h, self.col("b1_%d" % i, 8, 16), 0.5, ALU.mult)
            k.dma(onesb, V(self.csrc, I["c_onesb"]))
            for j in range(8):
                k.memset(Y[j][:, 0:30], 0.0, eng="pool")
                for t in range(31):
                    k.ts(Dg[:, j * 31 + t, :], self.identb, self.col("wdw_%d" % i, j * 31 + t, j * 31 + t + 1),
                         ALU.mult)

            tl = self.tiles(TT // 128)
            pre_a, pre_b = self.make_pre(tl, src, hA, ub, smA, uT, pT)

            def agA(ti, j):
                tt = len(tl[ti]) * 128
                u = uT[ti % 2]
                a_ = pag[j % 2][:, 0:tt]
                g_ = pag[j % 2][:, 256:256 + tt]
                k.mm(a_, [(W1[:, kc, j * 128:(j + 1) * 128], u[:, kc, 0:tt]) for kc in range(8)])
                k.mm(g_, [(W1[:, kc, D + j * 128:D + (j + 1) * 128], u[:, kc, 0:tt]) for kc in range(8)])
                k.act(sg[j % 2][:, 0:tt], g_, AF.Tanh, bias=b1h[:, j:j + 1], scale=0.5)

            def agB(ti, j):
                tt = len(tl[ti]) * 128
                a_ = pag[j % 2][:, 0:tt]
                s_ = sg[j % 2][:, 0:tt]
                k.ts(s_, s_, 0.5, ALU.mult, 0.5, ALU.add)
                k.stt(Y[j][:, 30:30 + tt], a_, self.col("b1_%d" % i, j, j + 1), s_, ALU.add, ALU.mult)
                if ti == 0:
                    k.memset(Y[j][:, 30:30 + 112], 0.0, eng="dve")

            def cv(ti, j):
                tt = len(tl[ti]) * 128
                c_ = pcv[j % 2][:, 0:tt]
                k.mm(c_, [(Dg[:, j * 31 + t, :], Y[j][:, t:t + tt]) for t in range(31)])
                bd = self.col("bdw_%d" % i, j, j + 1)
                k.act(Z[:, j, 0:tt], c_, AF.Identity, bias=bd)
                k.act(zb[:, j, 0:tt], c_, AF.Identity, bias=bd)
                k.act(zq[:, j, 0:tt], c_, AF.Square, bias=bd)
                k.copy(Y[j][:, 0:30], Y[j][:, tt:tt + 30], eng="dve")

            def prea_steps(ti):
                S_ = []
                for bi, b in enumerate(tl[ti]):
                    h, u_, sm = hA[bi % 2], ub[bi % 2], smA[bi % 2]
                    ss, t1, rs = sm[:, 0:1], sm[:, 1:2], sm[:, 2:3]

                    def p1(h=h, b=b, ss=ss):
                        k.dma(h, src[b])
                        k.act(self.junk, h, AF.Square, accum=ss)
                    S_.append(p1)
                    S_.append(lambda ss=ss, t1=t1, rs=rs: self.rstd(rs, ss, 1.0 / D, t1))
                    S_.append(lambda h=h, u_=u_, rs=rs: k.ts(u_, h, rs, ALU.mult))
                return S_

            def post_steps(ti):
                S_ = []
                for bi, b in enumerate(tl[ti]):
                    h, y_, tm, sm = hB[b % 2], yb[b % 2], tmp[b % 2], smB[b % 2]
                    ss0, ss1, ss, t1, rs = (sm[:, x:x + 1] for x in range(5))

                    def q1(y_=y_, ss0=ss0, ss1=ss1):
                        k.act(self.junk[:, 0:512], y_[:, 0:512], AF.Square, accum=ss0)
                        k.act(self.junk[:, 512:1024], y_[:, 512:1024], AF.Square, accum=ss1)

                    def q2(b=b, ss0=ss0, ss1=ss1, ss=ss, t1=t1, rs=rs):
                        k.tt(ss, ss0, ss1, ALU.add)
                        self.rstd(rs, ss, 1.0 / D, t1)
                        if b == 0:
                            k.tt(rs, rs, self.valid, ALU.mult)

                    def q3(y_=y_, tm=tm, rs=rs):
                        for hf in range(2):
                            cs = slice(hf * 512, (hf + 1) * 512)
                            k.stt(tm[:, cs], y_[:, cs], rs, gB[:, cs], ALU.mult, ALU.mult)

                    def q4(b=b, h=h, tm=tm):
                        k.tt(h, h, tm, ALU.add)
                        if dst[b] is not None:
                            k.dma(dst[b], h, q="pool")
                    S_ += [q1, q2, q3, q4]
                return S_

            pre_a(0)
            pre_b(0)
            for j in range(8):
                agA(0, j)
                agB(0, j)
            pending = []
            for ti, blocks in enumerate(tl):
                tt = len(blocks) * 128
                nxt = ti + 1 < len(tl)
                steps = pending + (prea_steps(ti + 1) if nxt else [])
                ns = len(steps)
                si = 0
                for j in range(8):
                    cv(ti, j)
                    upto = (ns * (j + 1)) // 8
                    while si < upto:
                        steps[si]()
                        si += 1
                if nxt:
                    pre_b(ti + 1)
                s1 = pss[:, 0:tt]
                s2 = pss[:, 256:256 + tt]
                k.mm(s1, [(onesb, zb[:, j, 0:tt]) for j in range(8)])
                k.mm(s2, [(onesb, zq[:, j, 0:tt]) for j in range(8)])
                mt, qt, vt, rt = m_[:, 0:tt], msq[:, 0:tt], var[:, 0:tt], rs_[:, 0:tt]
                k.act(mt, s1, AF.Copy, scale=1.0 / D)
                k.tt(qt, mt, mt, ALU.mult)
                k.stt(vt, s2, 1.0 / D, qt, ALU.mult, ALU.subtract)
                k.ts(vt, vt, 0.0, ALU.max)
                self.rstd(rt, vt, 1.0, qt)
                if nxt:
                    agA(ti + 1, 0)
                for j in range(8):
                    if nxt and j + 1 < 8:
                        agA(ti + 1, j + 1)
                    t_ = tz[j % 2][:, 0:tt]
                    k.tt(t_, Z[:, j, 0:tt], mt, ALU.subtract)
                    k.tt(t_, t_, rt, ALU.mult)
                    k.act(Nn[:, j, 0:tt], t_, AF.Silu, bias=self.col("lnb_%d" % i, j, j + 1),
                          scale=self.col("lng_%d" % i, j, j + 1))
                    if nxt:
                        agB(ti + 1, j)
                for bi, b in enumerate(blocks):
                    k.dma(hB[b % 2], src[b])
                    for hf in range(2):
                        cs = slice(hf * 512, (hf + 1) * 512)
                        k.mm(py[hf], [(Nn[:, j, bi * 128:(bi + 1) * 128], W2[:, j, cs]) for j in range(8)])
                        k.tt(yb[b % 2][:, cs], py[hf], b2B[:, cs], ALU.add)
                pending = post_steps(ti)
            for f_ in pending:
                f_()

    def phase_mixA(self, L, src, TT=256):
        k, I = self.k, self.I
        i = L // 2
        NP = self.NP
        with ExitStack() as st:
            Wi = k.sb(st, "Wi", [128, 8, INW], BF16)
            with ExitStack() as ws:
                self.wstage(ws)
                self.load_w(Wi, I["mix_w_in"][i], 8, INW, self.gcols(L, 0))
            self.S.barrier()
            trif = k.sb(st, "trif", [128, 128], F32)
            onesf = k.sb(st, "onesf", [128, 128], F32)
            maskT = k.sb(st, "maskT", [128, 128], BF16)
            gbB = k.sb(st, "gbB", [128, 8], F32)
            uT = [k.sb(st, "uT%d" % x, [128, 8, TT], BF16) for x in range(2)]
            X = [k.sb(st, "X%d" % j, [128, 4 + TT], BF16) for j in range(8)]
            Dq = k.sb(st, "Dq", [128, 32, 128], BF16)
            qk = [k.sb(st, "qk%d" % x, [128, 8, TT], BF16) for x in range(2)]
            qsks = [k.sb(st, "qsks%d" % x, [128, 8, TT], BF16) for x in range(2)]
            Vp = [k.sb(st, "Vp%d" % x, [128, 4, 260], BF16) for x in range(4)]
            sigo = [k.sb(st, "sigo%d" % x, [128, D], BF16) for x in range(4)]
            vst = [k.sb(st, "vst%d" % x, [128, 512], BF16) for x in range(2)]
            mixed = [k.sb(st, "mixed%d" % x, [128, D], BF16) for x in range(2)]
            C = [k.sb(st, "C%d" % h, [128, 260], F32) for h in range(4)]
            Cb = [k.sb(st, "Cb%d" % h, [128, 260], BF16) for h in range(4)]
            AT = [k.sb(st, "AT%d" % x, [128, 128], BF16) for x in range(4)]
            kd = [k.sb(st, "kd%d" % x, [128, 128], BF16) for x in range(4)]
            Pf = [k.sb(st, "Pf%d" % x, [128, 256], F32) for x in range(4)]
            hA = [k.sb(st, "hA%d" % x, [128, D], F32) for x in range(2)]
            ub = [k.sb(st, "ub%d" % x, [128, D], BF16) for x in range(2)]
            smA = [k.sb(st, "smA%d" % x, [128, 8], F32) for x in range(2)]
            gs_ = [k.sb(st, "gs%d" % x, [128, 64], F32) for x in range(4)]
            hs_ = [k.sb(st, "hsm%d" % x, [128, 48], F32) for x in range(2)]
            pT = k.ps(st, "pT", [128, 8, 128], BF16)
            pK = k.ps(st, "pK", [128, 8, 128], BF16)
            pp = [k.ps(st, "pp%d" % x, [128, 512], F32) for x in range(2)]
            pS = k.ps(st, "pS", [128, 4, 128], F32)
            pG = k.ps(st, "pG", [128, 512], F32)
            PU = [k.ps(st, "PU%d" % x, [128, 512], F32) for x in range(2)]

            k.dma(trif, V(self.csrc, I["c_trif"]))
            k.dma(onesf, V(self.csrc, I["c_onesf"]))
            k.dma(maskT, V(self.csrc, I["c_maskT"]))
            self.bcast_row(gbB, 18 + i, 8)
            for j in range(8):
                k.memset(X[j][:, 0:3], 0.0, eng="pool")
                for t in range(4):
                    k.ts(Dq[:, j * 4 + t, :], self.identb, self.col("cw_%d" % i, j * 4 + t, j * 4 + t + 1), ALU.mult)
            for h in range(4):
                k.memset(C[h], 0.0, eng="pool")
                k.memset(Cb[h], 0.0, eng="pool")
            for x in range(4):
                k.memset(Vp[x][:, :, 256:257], 1.0, eng="pool")
            cw = "cw_%d" % i
            ppi = [0]

            def proj_fm(col0, tt, u):
                p = pp[ppi[0] % 2][:, 0:tt]
                ppi[0] += 1
                k.mm(p, [(Wi[:, kc, col0:col0 + 128], u[:, kc, 0:tt]) for kc in range(8)])
                return p

            def proj_tm(col0, n, u, bi):
                p = pp[ppi[0] % 2][:, 0:n]
                ppi[0] += 1
                k.mm(p, [(u[:, kc, bi * 128:(bi + 1) * 128], Wi[:, kc, col0:col0 + n]) for kc in range(8)])
                return p

            tl = self.tiles(TT // 128)
            pre_a, pre_b = self.make_pre(tl, src, hA, ub, smA, uT, pT, eng="dve")

            def gsl(b):
                g = gs_[b % 4]
                return tuple(g[:, a:a + w] for a, w in (
                    (0, 8), (8, 8), (16, 4), (20, 4), (24, 8), (36, 4), (40, 4), (44, 4), (48, 4),
                    (52, 4), (56, 4), (60, 4)))

            def front(ti, bi, b):
                blocks = tl[ti]
                tt = len(blocks) * 128
                t0 = blocks[0] * 128
                u = uT[ti % 2]
                q_ = qk[ti % 2]
                qs_ = qsks[ti % 2]
                G = []
                if True:
                    def fm(j):
                        if j < 8:
                            p = proj_fm(j * 128, tt, u)
                            k.copy(X[j][:, 3:3 + tt], p, eng="dve")
                        if j > 0:
                            jj = j - 1
                            c = pp[ppi[0] % 2][:, 0:tt]
                            ppi[0] += 1
                            k.mm(c, [(Dq[:, jj * 4 + t, :], X[jj][:, t:t + tt]) for t in range(4)])
                            k.act(q_[:, jj, 0:tt], c, AF.Silu, bias=self.col("cb_%d" % i, jj, jj + 1))
                            k.copy(X[jj][:, 0:3], X[jj][:, tt:tt + 3], eng="dve")

                if True:
                    def qsk(j):
                        p = proj_fm(3080 + j * 128, tt, u)
                        k.copy(qs_[:, j, 0:tt], p, eng="dve")
                        if j == 7:
                            k.dma(V(self.b_q[blocks[0]], self.d_qs[:, :, t0:t0 + tt].rearrange("h p t -> p h t")),
                                  qs_[:, 0:4, 0:tt], q="pool")
                            k.dma(V(self.b_q[blocks[0]], self.d_ks[:, :, t0:t0 + tt].rearrange("h p t -> p h t")),
                                  qs_[:, 4:8, 0:tt], q="pool")
                if bi == 0:
                    for j in range(9):
                        G.append(lambda j=j: fm(j))
                if bi == len(blocks) - 1:
                    for j in range(8):
                        G.append(lambda j=j: qsk(j))
                lastb = bi == len(blocks) - 1 and ti + 1 < len(tl)
                if lastb:
                    G.insert(0, lambda: pre_a(ti + 1))
                bs = slice(bi * 128, (bi + 1) * 128)
                vp = Vp[b % 4]
                so = sigo[b % 4]
                g1, g2, ex, sp, nbG, e1, c_, r_, t3, e3, d_, eg = gsl(b)

                def gates1():
                    pg_ = pp[ppi[0] % 2][:, 0:8]
                    ppi[0] += 1
                    k.mm(pg_, [(u[:, kc, bs], Wi[:, kc, 3072:3080]) for kc in range(8)])
                    k.tt(g1, pg_, gbB, ALU.add)
                    k.act(g2, g1, AF.Tanh, scale=1.0 / 15.0)

                def gates1b():
                    k.act(ex, g2[:, 4:8], AF.Exp, scale=-15.0)
                    k.act(sp, ex, AF.Ln, bias=self.onec)
                    if b == 0:
                        k.ts(sp, sp, self.valid, ALU.mult)

                def vproj(hf):
                    p = proj_tm(1024 + hf * 512, 512, u, bi)
                    k.copy(vp[:, 2 * hf:2 * hf + 2, 0:256], p.re("p (a c) -> p a c", a=2), eng="dve")

                def gates2():
                    pc_ = pp[ppi[0] % 2]
                    ppi[0] += 1
                    k.mm(pc_[:, 0:4], [(trif, sp)])
                    k.mm(pc_[:, 4:8], [(onesf, sp)])
                    k.copy(nbG, pc_[:, 0:8], eng="dve")
                    k.stt(e1, g2[:, 0:4], 15.0, nbG[:, 0:4], ALU.mult, ALU.add)
                    k.act(c_, e1, AF.Exp)
                    k.act(r_, nbG[:, 0:4], AF.Exp, scale=-1.0, bias=self.lnq)
                    k.tt(t3, nbG[:, 0:4], nbG[:, 4:8], ALU.subtract)
                    k.stt(e3, g2[:, 0:4], 15.0, t3, ALU.mult, ALU.add)
                    k.act(d_, e3, AF.Exp)
                    k.act(eg, nbG[:, 4:8], AF.Exp, scale=-1.0)
                    if b == 0:
                        k.ts(c_, c_, self.valid, ALU.mult)
                        k.ts(d_, d_, self.valid, ALU.mult)

                def oproj(hf):
                    p = proj_tm(2048 + hf * 512, 512, u, bi)
                    k.act(so[:, hf * 512:(hf + 1) * 512], p, AF.Tanh, scale=0.5)

                def vsproj():
                    p = proj_tm(4104, 512, u, bi)
                    k.copy(vst[b % 2], p, eng="dve")
                    k.dma(V(self.b_q[b], self.d_vs[b * 128:(b + 1) * 128, :]), vst[b % 2], q="pool")

                G += [gates1, lambda: oproj(0), lambda: oproj(1), gates1b, lambda: vproj(0), lambda: vproj(1), gates2,
                      vsproj]
                if lastb:
                    G.append(lambda: pre_b(ti + 1))
                return G

            def heads(ti, bi, b):
                q_ = qk[ti % 2]
                bs = slice(bi * 128, (bi + 1) * 128)
                vp = Vp[b % 4]
                so = sigo[b % 4]
                g1, g2, ex, sp, nbG, e1, c_, r_, t3, e3, d_, eg = gsl(b)
                mx = mixed[b % 2]
                hs = hs_[b % 2]
                dd, ad, rec, fac, ssq, f2, msq, t4, rs, fs = (hs[:, 4 * a:4 * a + 4] for a in range(10))
                qT = [q_[:, h, bs] for h in range(4)]
                kT = [q_[:, 4 + h, bs] for h in range(4)]

                def s1():
                    for h in range(4):
                        k.mm(pS[:, h, :], [(kT[h], qT[h])])
                    k.transposes([(pK[:, h, :], kT[h]) for h in range(4)], self.identb)

                def s2():
                    for h in range(4):
                        k.stt(AT[h], pS[:, h, :], c_[:, h:h + 1], maskT, ALU.mult, ALU.mult)
                    for h in range(4):
                        k.act(kd[h], pK[:, h, :], AF.Copy, scale=d_[:, h:h + 1])

                def step3(h):
                    pu_ = PU[h % 2]
                    k.mm(pu_[:, 0:256], [(AT[h], vp[:, h, 0:256]), (qT[h], Cb[h][:, 0:256])])
                    k.mm(pu_[:, 256:512], [(kd[h], vp[:, h, 0:256])])
                    k.mm(pG[:, 16 + h:17 + h], [(AT[h], vp[:, h, 256:257]), (qT[h], Cb[h][:, 256:257])])
                    k.mm(pG[:, 20 + h:21 + h], [(kd[h], vp[:, h, 256:257])])

                def step4(h):
                    pu_ = PU[h % 2]
                    k.stt(C[h][:, 0:256], C[h][:, 0:256], eg[:, h:h + 1], pu_[:, 256:512], ALU.mult, ALU.add)
                    k.stt(C[h][:, 256:257], C[h][:, 256:257], eg[:, h:h + 1], pG[:, 20 + h:21 + h],
                          ALU.mult, ALU.add)
                    k.copy(Pf[h], pu_[:, 0:256], eng="dve")
                    k.copy(Cb[h][:, 0:257], C[h][:, 0:257], eng="dve")
                    k.act(self.junk[:, h * 256:(h + 1) * 256], Pf[h], AF.Square, accum=ssq[:, h:h + 1])

                def s7():
                    k.tt(dd, pG[:, 16:20], r_, ALU.mult)
                    k.act(ad, dd, AF.Abs)
                    k.ts(ad, ad, 1.0, ALU.max)
                    k.recip(rec, ad)
                    k.tt(fac, r_, rec, ALU.mult)
                    k.tt(f2, fac, fac, ALU.mult)
                    k.stt(msq, ssq, 1.0 / 256.0, f2, ALU.mult, ALU.mult)
                    self.rstd(rs, msq, 1.0, t4)
                    k.stt(fs, fac, 0.5, rs, ALU.mult, ALU.mult)

                def s8():
                    for h in range(4):
                        k.ts(Pf[h], Pf[h], fs[:, h:h + 1], ALU.mult)
                        k.stt(mx[:, h * 256:(h + 1) * 256], so[:, h * 256:(h + 1) * 256], 1.0, Pf[h],
                              ALU.add, ALU.mult)
                    k.dma(V(self.b_mm[b], self.d_mm[b * 128:(b + 1) * 128, :]), mx, q="pool")

                return [s1, s2, lambda: (step3(0), step3(1)), lambda: (step4(0), step4(1)),
                        lambda: (step3(2), step3(3)), lambda: (step4(2), step4(3)), s7, s8]

            order = [(ti, bi, b) for ti, blocks in enumerate(tl) for bi, b in enumerate(blocks)]
            pre_a(0)
            pre_b(0)
            for g in front(*order[0]):
                g()
            if len(order) > 1:
                for g in front(*order[1]):
                    g()
            for x, cur in enumerate(order):
                Fn = front(*order[x + 2]) if x + 2 < len(order) else []
                Hs = heads(*cur)
                nF, nH = len(Fn), len(Hs)
                fi = 0
                for hi, step in enumerate(Hs):
                    upto = (nF * (hi + 1)) // nH
                    while fi < upto:
                        Fn[fi]()
                        fi += 1
                    step()
                while fi < nF:
                    Fn[fi]()
                    fi += 1

    def phase_mixB(self, L, TQ=512):
        k, I = self.k, self.I
        NB, NP = self.NB, self.NP
        with ExitStack() as st:
            KT = k.sb(st, "KT", [128, 4, NP], BF16)
            Vt = k.sb(st, "Vt", [128, NB, 512], BF16)
            trib = k.sb(st, "trib", [128, 128], BF16)
            onesb = k.sb(st, "onesb", [128, 128], BF16)
            maskJ = k.sb(st, "maskJ", [128, 4, 512], BF16)
            qt = [k.sb(st, "qt%d" % x, [128, TQ], BF16) for x in range(2)]
            qn = [k.sb(st, "qn%d" % x, [128, TQ], BF16) for x in range(2)]
            ef = [k.sb(st, "ef%d" % x, [128, TQ], F32) for x in range(2)]
            spb = [k.sb(st, "spb%d" % x, [128, TQ], BF16) for x in range(3)]
            Rr = [k.sb(st, "Rr%d" % x, [128, TQ], F32) for x in range(2)]
            tX = [k.sb(st, "tX%d" % x, [128, TQ], F32) for x in range(3)]
            wT = [k.sb(st, "wT%d" % x, [128, TQ], BF16) for x in range(3)]
            ost = [k.sb(st, "ost%d" % x, [128, TQ], BF16) for x in range(2)]
            pz = [k.ps(st, "pz%d" % x, [128, 512], F32) for x in range(2)]
            pX = [k.ps(st, "pX%d" % x, [128, 512], F32) for x in range(2)]
            pc = [k.ps(st, "pc%d" % x, [128, 512], F32) for x in range(2)]
            pO = [k.ps(st, "pO%d" % x, [128, 512], F32) for x in range(2)]

            k.dma(trib, V(self.csrc, I["c_trib"]))
            k.dma(onesb, V(self.csrc, I["c_onesb"]))
            k.dma(maskJ, V(self.csrc, I["c_maskJ"].rearrange("p (j t) -> p j t", j=4)))
            allq = [V(b, None) for b in self.b_q]
            KTh = [V(Buf("KT%d" % h), KT.ap[:, h, :]) for h in range(4)]
            for h in range(4):
                k.S.op("sp", (lambda e, h=h: e.dma_start(out=KT.ap[:, h, :], in_=self.d_ks[h])), allq, [KTh[h]],
                       dma=True)
            Vth = []
            nq = 4
            for x in range(nq):
                n0, n1 = (NB * x) // nq, (NB * (x + 1)) // nq
                vb = V(Buf("Vt%d" % x), None)
                Vth.append((n0, n1, vb))
                if n1 > n0:
                    k.S.op("sp", (lambda e, n0=n0, n1=n1: e.dma_start(
                        out=Vt.ap[:, n0:n1, :],
                        in_=self.d_vs[n0 * 128:n1 * 128, :].rearrange("(n p) c -> p n c", p=128))),
                        allq, [vb], dma=True)

            def vbuf(n):
                for n0, n1, vb in Vth:
                    if n0 <= n < n1:
                        return vb

            units = []
            qi = 0
            qtiles = [[0]] + [list(range(t, min(NB, t + TQ // 128))) for t in range(1, NB, TQ // 128)]
            for ti, blocks in enumerate(qtiles):
                for h in range(4):
                    nmax = blocks[-1]
                    for n in range(nmax, -1, -1):
                        units.append(dict(ti=ti, blocks=blocks, h=h, n=n, first=(n == nmax), last=(n == 0), qi=qi))
                    qi += 1

            def common(u, ui):
                blocks, h, n = u["blocks"], u["h"], u["n"]
                nc_ = len(blocks) * 128
                c0 = max(0, n - blocks[0]) * 128
                kb = V(KTh[h].buf, KT.ap[:, h, n * 128:(n + 1) * 128])
                return blocks, h, n, c0, nc_, blocks[0] * 128, qt[u["qi"] % 2], qn[u["qi"] % 2], kb

            def stageA1(u, ui):
                blocks, h, n, c0, nc_, t0, q, qm, kb = common(u, ui)
                if u["first"]:
                    k.S.op("sp", (lambda e, o_=q.ap[:, 0:nc_], i_=self.d_qs[h, :, t0:t0 + nc_]:
                                  e.dma_start(out=o_, in_=i_)),
                           [V(self.b_q[bb], None) for bb in blocks], [q], dma=True)
                    k.ts(qm[:, 0:nc_], q[:, 0:nc_], -SB_SCALE, ALU.mult, eng="dve")
                z = pz[ui % 2][:, c0:nc_]
                k.mm(z, [(kb, q[:, c0:nc_])])
                k.act(z, z, AF.Exp, scale=SB_SCALE)

            def stageA2(u, ui):
                blocks, h, n, c0, nc_, t0, q, qm, kb = common(u, ui)
                s_ = spb[ui % 3][:, c0:nc_]
                k.act(s_, pz[ui % 2][:, c0:nc_], AF.Ln, bias=self.onec)
                if n >= blocks[0]:
                    k.tt(s_, s_, maskJ[:, n - blocks[0], c0:nc_], ALU.mult)
                if n == 0:
                    k.ts(s_, s_, self.valid, ALU.mult)

            def stageB1(u, ui):
                blocks, h, n, c0, nc_, t0, q, qm, kb = common(u, ui)
                first, last = u["first"], u["last"]
                s_ = spb[ui % 3][:, c0:nc_]
                Xp = pX[ui % 2][:, c0:nc_]
                cp = pc[ui % 2][:, c0:nc_]
                ci = blocks[-1] - n
                Rcur = Rr[ci % 2]
                Rnew = Rr[(ci + 1) % 2]
                k.mm(Xp, [(trib, s_), (kb, qm[:, c0:nc_])])
                if not last:
                    k.mm(cp, [(onesb, s_)])
                if first:
                    if not last:
                        k.copy(Rnew[:, c0:nc_], cp, eng="dve")
                else:
                    k.tt(Xp, Xp, Rcur[:, c0:nc_], ALU.add)
                    if not last:
                        k.tt(Rnew[:, c0:nc_], cp, Rcur[:, c0:nc_], ALU.add)
                if c0 > 0 and not last:
                    k.memset(Rnew[:, 0:c0], 0.0, eng="dve")

            def stageB2(u, ui):
                blocks, h, n, c0, nc_, t0, q, qm, kb = common(u, ui)
                w_ = wT[ui % 3][:, c0:nc_]
                k.act(w_, pX[ui % 2][:, c0:nc_], AF.Exp, scale=-1.0)
                if n >= blocks[0]:
                    k.tt(w_, w_, maskJ[:, n - blocks[0], c0:nc_], ALU.mult)

            def stageC(u, ui):
                blocks, h, n, c0, nc_, t0, q, qm, kb = common(u, ui)
                w_ = wT[ui % 3][:, c0:nc_]
                O = pO[u["qi"] % 2]
                vt = V(vbuf(n).buf, Vt.ap[:, n, h * 128:(h + 1) * 128])
                k.mm(O[:, c0:nc_], [(vt, w_)], start=u["first"], stop=u["last"])
                if u["last"]:
                    o_ = ost[u["qi"] % 2]
                    k.copy(o_[:, 0:nc_], O[:, 0:nc_], eng="dve")
                    k.dma(V(self.b_hs[blocks[0]], self.d_hs[h, :, t0:t0 + nc_]), o_[:, 0:nc_], q="pool")

            nu = len(units)
            for i in range(nu + 3):
                if i < nu:
                    stageA1(units[i], i)
                if 0 <= i - 2 < nu:
                    stageB2(units[i - 2], i - 2)
                if i < nu:
                    stageA2(units[i], i)
                if 0 <= i - 1 < nu:
                    stageB1(units[i - 1], i - 1)
                if 0 <= i - 3 < nu:
                    stageC(units[i - 3], i - 3)

    def phase_mixC(self, L, src, dst):
        k, I = self.k, self.I
        i = L // 2
        NB = self.NB
        with ExitStack() as st:
            Wo = k.sb(st, "Wo", [128, 12, D], BF16)
            gB = k.sb(st, "gB", [128, D], F32)
            self.bcast_row(gB, L * 4 + 1)
            hng = [self.col("hng_%d" % i, c, c + 1) for c in range(8)] + [None] * 4
            with ExitStack() as ws:
                self.wstage(ws)
                self.load_w(Wo, I["mix_w_out"][i], 12, D, hng)
            self.S.barrier()
            mmi = [k.sb(st, "mmi%d" % x, [128, D], BF16) for x in range(2)]
            mmT = [k.sb(st, "mmT%d" % x, [128, 8, 128], BF16) for x in range(2)]
            hsT = [k.sb(st, "hsT%d" % x, [128, 4, 128], BF16) for x in range(2)]
            hB = [k.sb(st, "hB%d" % x, [128, D], F32) for x in range(2)]
            tmp = k.sb(st, "tmp", [128, D], F32)
            smB = [k.sb(st, "smB%d" % x, [128, 8], F32) for x in range(2)]
            pT = [k.ps(st, "pT%d" % x, [128, 8, 128], BF16) for x in range(2)]
            py = [k.ps(st, "py%d" % x, [128, 512], F32) for x in range(4)]
            tmp2 = [tmp, k.sb(st, "tmpb", [128, D], F32)]

            def post_steps(b):
                x = b % 2
                h, sm, tm = hB[x], smB[x], tmp2[x]
                yy = [py[2 * x], py[2 * x + 1]]
                ss0, ss1, ss, t1, rs = (sm[:, c:c + 1] for c in range(5))

                def q1():
                    k.act(self.junk[:, 0:512], yy[0], AF.Square, accum=ss0)
                    k.act(self.junk[:, 512:1024], yy[1], AF.Square, accum=ss1)

                def q2():
                    k.tt(ss, ss0, ss1, ALU.add)
                    self.rstd(rs, ss, 1.0 / D, t1)
                    if b == 0:
                        k.tt(rs, rs, self.valid, ALU.mult)

                def q3():
                    for hf in range(2):
                        cs = slice(hf * 512, (hf + 1) * 512)
                        k.stt(tm[:, cs], yy[hf], rs, gB[:, cs], ALU.mult, ALU.mult)

                def q4():
                    k.tt(h, h, tm, ALU.add)
                    if dst[b] is not None:
                        k.dma(dst[b], h, q="pool")
                return [q1, q2, q3, q4]

            pend = [lambda: None] * 4
            for b in range(NB):
                x = b % 2
                k.dma(mmi[x], V(self.b_mm[b], self.d_mm[b * 128:(b + 1) * 128, :]))
                k.S.op("sp", (lambda e, x=x, b=b: e.dma_start(
                    out=hsT[x].ap, in_=self.d_hs[:, :, b * 128:(b + 1) * 128].rearrange("h p t -> p h t"))),
                    [V(self.b_hs[self.hs_first[b]], None)], [hsT[x]], dma=True)
                k.dma(hB[x], src[b])
                k.transposes([(pT[x][:, c, :], mmi[x][:, c * 128:(c + 1) * 128]) for c in range(8)], self.identb)
                pend[0]()
                k.copy(mmT[x], pT[x], eng="act")
                pend[1]()
                yy = [py[2 * x], py[2 * x + 1]]
                for hf in range(2):
                    cs = slice(hf * 512, (hf + 1) * 512)
                    k.mm(yy[hf], [(mmT[x][:, c, :], Wo[:, c, cs]) for c in range(8)] +
                         [(hsT[x][:, hh, :], Wo[:, 8 + hh, cs]) for hh in range(4)])
                    pend[2 + hf]()
                pend = post_steps(b)
            for f_ in pend:
                f_()


def colT(v):
    v = np.asarray(v, np.float32)
    return np.ascontiguousarray(v.reshape(-1, 128).T)


def host_consts():
    bf = ml_dtypes.bfloat16
    c = {}
    r = np.arange(128)
    c["c_identb"] = np.eye(128, dtype=np.float32).astype(bf)
    c["c_onesb"] = np.ones((128, 128), np.float32).astype(bf)
    c["c_trib"] = (r[:, None] >= r[None, :]).astype(np.float32).astype(bf)
    c["c_maskT"] = (r[:, None] <= r[None, :]).astype(np.float32).astype(bf)
    t = np.arange(512)
    mj = np.stack([((128 * j + r[:, None]) < t[None, :]).astype(np.float32) for j in range(4)], axis=1)
    c["c_maskJ"] = np.ascontiguousarray(mj.reshape(128, 4 * 512)).astype(bf)
    c["c_trif"] = (r[:, None] <= r[None, :]).astype(np.float32)
    c["c_onesf"] = np.ones((128, 128), np.float32)
    return c


def prep_shared(inp):
    sh = dict(host_consts())
    f = lambda n: np.asarray(inp[n], np.float32)
    ng = f("norm_g")
    cols = np.zeros((128, NCOLS), np.float32)

    def put(name, arr):
        o, w = COLS[name]
        assert arr.shape == (128, w), (name, arr.shape, w)
        cols[:, o:o + w] = arr

    put("ng", np.concatenate([colT(ng[l, j]) for l in range(4) for j in range(4)], axis=1))
    put("valid", (np.arange(128) >= 112).astype(np.float32)[:, None])
    for i in range(2):
        put("b1_%d" % i, colT(f("conv_b_pw1")[i]))
        wdw = f("conv_w_dw")[i]
        put("wdw_%d" % i, np.ascontiguousarray(wdw.reshape(31, 8, 128).transpose(2, 1, 0)).reshape(128, 248))
        put("bdw_%d" % i, colT(f("conv_b_dw")[i]))
        put("lng_%d" % i, colT(f("conv_ln_g")[i]))
        put("lnb_%d" % i, colT(f("conv_ln_b")[i]))
        cw = f("mix_qk_conv_w")[i]
        put("cw_%d" % i, np.ascontiguousarray(cw.reshape(4, 8, 128).transpose(2, 1, 0)).reshape(128, 32))
        put("cb_%d" % i, colT(f("mix_qk_conv_b")[i]))
        put("hng_%d" % i, colT(f("mix_hnorm_g")[i]))
    sh["colsT"] = cols
    rows = np.zeros((NROWS, D), np.float32)
    rows[0:16] = ng.reshape(16, D)
    rows[16:18] = f("conv_b_pw2")
    rows[18:20, 0:8] = f("mix_gate_b")
    sh["rows"] = rows
    for n in ("ffn_w_gate", "ffn_w_up", "ffn_w_down", "conv_w_pw1", "conv_w_pw2", "mix_w_in", "mix_w_out"):
        sh[n] = np.ascontiguousarray(f(n))
    return sh


def make_hp(x_b, meta, NB):
    hp = np.zeros((NB * 128, D), np.float32)
    hp[112:128] = meta
    hp[128:] = x_b
    return hp


ALL_LAYERS = [("M", 0), ("F", 0), ("V", 1), ("F", 1), ("M", 2), ("F", 2), ("V", 3), ("F", 3)]


def run(inp, NB, layers, n_cores=8, trace=False, debug=False):
    x = np.asarray(inp["x"], np.float32)
    meta = np.asarray(inp["meta"], np.float32)
    prog = Prog(NB, layers, debug=debug)
    nc = prog.build()
    sh = prep_shared(inp)
    in_maps = []
    for c in range(n_cores):
        m = dict(sh)
        m["hp"] = make_hp(x[c], meta, NB)
        in_maps.append(m)
    res = run_bass_kernel_spmd(nc, in_maps, core_ids=list(range(n_cores)), trace=trace)
    out = np.stack([np.asarray(r["out"], np.float32) for r in res.results], axis=0)
    return out, res


def kernel(**inputs):
    out, _ = run(inputs, 65, ALL_LAYERS)
    return out
```
